# Optimizing a Trainium2 kernel written in Bass

```python
import math
import jax, jax.numpy as jnp
from jax import lax
import numpy as np

D_MODEL = 1024
BATCH = 16
SEQ = 256
DEPTH = 2
DEC_BATCH = 2
DEC_SEQ = 4096
PAST_LEN = 256

GRID_W = 64
N_ATT = (DEPTH + 1) // 2
N_CONV = DEPTH // 2
H_DIFF = 4
DH_DIFF = 64
DIFF_QK = H_DIFF * 2 * DH_DIFF
DIFF_V = H_DIFF * 2 * DH_DIFF
H_MLA = 8
Q_LORA = 256
KV_LORA = 128
QK_NOPE = 64
QK_ROPE = 32
V_MLA = 64
ATT_IN = 2 * DIFF_QK + DIFF_V + Q_LORA + KV_LORA + QK_ROPE
ATT_SPLITS = [DIFF_QK, 2 * DIFF_QK, 2 * DIFF_QK + DIFF_V, 2 * DIFF_QK + DIFF_V + Q_LORA,
              2 * DIFF_QK + DIFF_V + Q_LORA + KV_LORA]
ATT_OUT = DIFF_V + H_MLA * V_MLA
CONV_CH = 512
CONV_WIDTH = 31
POOL_CH = 512
POOL_WINDOWS = (2, 4, 8, 16)
POOL_GROUPS = len(POOL_WINDOWS)
POOL_GC = POOL_CH // POOL_GROUPS
CONV_IN = 2 * CONV_CH + POOL_CH
CONV_OUT = CONV_CH + POOL_CH
D_FF = 4 * D_MODEL
ROPE_BASE = 10000.0
Q_BLOCK = 128
NORM_EPS = 1e-5
DEEPNORM_ALPHA = (2 * DEPTH) ** 0.25
DEEPNORM_BETA = (8 * DEPTH) ** -0.25

kernel_name = 'hybrid_diffattn_mla_conformer_pool_dit'


def layer_norm(x, g, b):
    xf = x.astype(jnp.float32)
    mu = jnp.mean(xf, axis=-1, keepdims=True)
    var = jnp.mean(jnp.square(xf - mu), axis=-1, keepdims=True)
    return ((xf - mu) * lax.rsqrt(var + NORM_EPS) * g.astype(jnp.float32) + b.astype(jnp.float32)).astype(x.dtype)


def rms_norm(x, g):
    xf = x.astype(jnp.float32)
    ms = jnp.mean(jnp.square(xf), axis=-1, keepdims=True)
    return (xf * lax.rsqrt(ms + NORM_EPS) * g.astype(jnp.float32)).astype(x.dtype)


def rope_1d(x, pos):
    half = x.shape[-1] // 2
    inv = ROPE_BASE ** (-jnp.arange(half, dtype=jnp.float32) / half)
    ang = pos.astype(jnp.float32)[:, None] * inv[None, :]
    ang = ang.reshape(ang.shape[0], *([1] * (x.ndim - 3)), half)
    cos, sin = jnp.cos(ang), jnp.sin(ang)
    xf = x.astype(jnp.float32)
    x1, x2 = xf[..., :half], xf[..., half:]
    return jnp.concatenate([x1 * cos - x2 * sin, x1 * sin + x2 * cos], axis=-1).astype(x.dtype)


def rope_2d(x, row, col):
    n = x.shape[-1] // 2
    return jnp.concatenate([rope_1d(x[..., :n], row), rope_1d(x[..., n:], col)], axis=-1)


def sweep_query_blocks(fn, qs):
    b, t = qs[0].shape[:2]
    qb = Q_BLOCK if t % Q_BLOCK == 0 else t
    nb = t // qb
    blocks = tuple(jnp.moveaxis(q.reshape(b, nb, qb, *q.shape[2:]), 1, 0) for q in qs)
    out = lax.map(lambda blk: fn(*blk), blocks)
    out = jnp.moveaxis(out, 0, 1)
    return out.reshape(b, t, *out.shape[3:])


def diff_attend(q_segs, k_segs, v, lam):
    scale = DH_DIFF ** -0.5

    def blk(*qs):
        s = jnp.concatenate([jnp.einsum('bqhmd,bkhmd->bhmqk', q, k) for q, k in zip(qs, k_segs)], axis=-1)
        p = jax.nn.softmax(s.astype(jnp.float32) * scale, axis=-1)
        a = p[:, :, 0] - lam * p[:, :, 1]
        return jnp.einsum('bhqk,bkhe->bqhe', a, v.astype(jnp.float32)).astype(v.dtype)

    return sweep_query_blocks(blk, tuple(q_segs))


def mla_attend(q_nope, pe_q, k_nope, pe_k, v):
    scale = (QK_NOPE + QK_ROPE) ** -0.5

    def blk(qn, *qps):
        s_pe = jnp.concatenate([jnp.einsum('bqhr,bkr->bhqk', qp, kp) for qp, kp in zip(qps, pe_k)], axis=-1)
        s = jnp.einsum('bqhd,bkhd->bhqk', qn, k_nope) + s_pe
        p = jax.nn.softmax(s.astype(jnp.float32) * scale, axis=-1)
        return jnp.einsum('bhqk,bkhe->bqhe', p, v.astype(jnp.float32)).astype(v.dtype)

    return sweep_query_blocks(blk, (q_nope,) + tuple(pe_q))


def attention_mixer(h, w_in, w_uq, w_ukv, q_norm_g, kv_norm_g, lam, lam_init, diff_norm_g, w_out, ctx, pos):
    b, t, _ = h.shape
    q_d, k_d, v_d, cq, ckv, kpe = jnp.split(h @ w_in, ATT_SPLITS, axis=-1)
    q_d = q_d.reshape(b, t, H_DIFF, 2, DH_DIFF)
    k_d = k_d.reshape(b, t, H_DIFF, 2, DH_DIFF)
    v_d = v_d.reshape(b, t, H_DIFF, 2 * DH_DIFF)
    q_m = (rms_norm(cq, q_norm_g) @ w_uq).reshape(b, t, H_MLA, QK_NOPE + QK_ROPE)
    q_nope, q_pe = q_m[..., :QK_NOPE], q_m[..., QK_NOPE:]
    ckv = rms_norm(ckv, kv_norm_g)
    if ctx is None:
        v_all, ckv_all = v_d, ckv
        diff_q, diff_k = (q_d,), (k_d,)
        pe_q, pe_k = (q_pe,), (kpe,)
    else:
        row, col = pos
        k_ctx = ctx[0].reshape(b, -1, H_DIFF, 2, DH_DIFF)
        v_all = jnp.concatenate([ctx[1], v_d], axis=1)
        ckv_all = jnp.concatenate([ctx[2], ckv], axis=1)
        diff_q, diff_k = (q_d, rope_2d(q_d, row, col)), (k_ctx, rope_2d(k_d, row, col))
        pe_q, pe_k = (q_pe, rope_2d(q_pe, row, col)), (ctx[3], rope_2d(kpe, row, col))
    o_d = diff_attend(diff_q, diff_k, v_all, lam)
    kv = (ckv_all @ w_ukv).reshape(b, -1, H_MLA, QK_NOPE + V_MLA)
    o_m = mla_attend(q_nope, pe_q, kv[..., :QK_NOPE], pe_k, kv[..., QK_NOPE:])
    o_d = (rms_norm(o_d, diff_norm_g) * (1.0 - lam_init)).reshape(b, t, DIFF_V)
    y = jnp.concatenate([o_d, o_m.reshape(b, t, H_MLA * V_MLA)], axis=-1) @ w_out
    new_ctx = (k_d.reshape(b, t, H_DIFF, 2 * DH_DIFF), v_d, ckv, kpe)
    return y, new_ctx


def depthwise_conv(u, w, bias):
    out = lax.conv_general_dilated(u, w[:, None, :].astype(u.dtype), window_strides=(1,),
                                   padding=[(CONV_WIDTH // 2, CONV_WIDTH // 2)],
                                   dimension_numbers=('NWC', 'WIO', 'NWC'),
                                   feature_group_count=u.shape[-1])
    return out + bias


def multi_scale_pool(x):
    t_len = x.shape[1]
    t = jnp.arange(t_len)
    xf = x.astype(jnp.float32)
    cs = jnp.concatenate([jnp.zeros_like(xf[:, :1]), lax.cumsum(xf, axis=1)], axis=1)
    outs = []
    for g, w in enumerate(POOL_WINDOWS):
        lo = jnp.maximum(t - w // 2, 0)
        hi = jnp.minimum(t + w // 2 - 1, t_len - 1)
        csg = cs[..., g * POOL_GC:(g + 1) * POOL_GC]
        cnt = (hi - lo + 1).astype(jnp.float32)[None, :, None]
        outs.append((csg[:, hi + 1] - csg[:, lo]) / cnt)
    return jnp.concatenate(outs, axis=-1).astype(x.dtype)


def conv_pool_mixer(h, w_in, conv_w, conv_b, cn_g, cn_b, w_pool, pool_scale, w_out):
    b, t, _ = h.shape
    a, gate, pz = jnp.split(h @ w_in, [CONV_CH, 2 * CONV_CH], axis=-1)
    u = a * jax.nn.sigmoid(gate)
    u = jax.nn.silu(layer_norm(depthwise_conv(u, conv_w, conv_b), cn_g, cn_b))
    d = (multi_scale_pool(pz) - pz).reshape(b, t, POOL_GROUPS, POOL_GC)
    d = jnp.einsum('btgc,gce->btge', d, w_pool).reshape(b, t, POOL_CH) * pool_scale
    return jnp.concatenate([u, d], axis=-1) @ w_out


def squared_relu_mlp(h, w1, w2):
    return jnp.square(jax.nn.relu(h @ w1)) @ w2


def setup_inputs(seed: int = 0) -> dict:
    key = jax.random.key(seed)
    ks = jax.random.split(key, 40)
    cnt = [0]

    def nrm(shape, scale):
        k_ = ks[cnt[0]]
        cnt[0] += 1
        return scale * jax.random.normal(k_, shape, jnp.float32)

    d = D_MODEL
    return {
        'x_prompt': nrm((BATCH, SEQ, d), 1.0),
        'x_sample': nrm((DEC_BATCH, DEC_SEQ, d), 1.0),
        'cache_diff_k': nrm((DEC_BATCH, N_ATT, PAST_LEN, H_DIFF, 2 * DH_DIFF), 1.0),
        'cache_diff_v': nrm((DEC_BATCH, N_ATT, PAST_LEN, H_DIFF, 2 * DH_DIFF), 1.0),
        'cache_mla_ckv': nrm((DEC_BATCH, N_ATT, PAST_LEN, KV_LORA), 1.0),
        'cache_mla_krope': nrm((DEC_BATCH, N_ATT, PAST_LEN, QK_ROPE), 1.0),
        'c': nrm((DEC_BATCH, d), 1.0),
        'c_ctx': nrm((d,), 1.0),
        'w_ada': nrm((DEPTH, d, 6 * d), d ** -0.5),
        'b_ada': nrm((DEPTH, 6 * d), 0.02),
        'ln_mix_g': 1.0 + nrm((DEPTH, d), 0.02),
        'ln_mix_b': nrm((DEPTH, d), 0.02),
        'ln_mlp_g': 1.0 + nrm((DEPTH, d), 0.02),
        'ln_mlp_b': nrm((DEPTH, d), 0.02),
        'w_mlp_in': nrm((DEPTH, d, D_FF), d ** -0.5),
        'w_mlp_out': nrm((DEPTH, D_FF, d), D_FF ** -0.5 * DEEPNORM_BETA),
        'w_att_in': nrm((N_ATT, d, ATT_IN), d ** -0.5),
        'w_uq': nrm((N_ATT, Q_LORA, H_MLA * (QK_NOPE + QK_ROPE)), Q_LORA ** -0.5),
        'w_ukv': nrm((N_ATT, KV_LORA, H_MLA * (QK_NOPE + V_MLA)), KV_LORA ** -0.5),
        'q_norm_g': 1.0 + nrm((N_ATT, Q_LORA), 0.02),
        'kv_norm_g': 1.0 + nrm((N_ATT, KV_LORA), 0.02),
        'lam_q1': nrm((N_ATT, DH_DIFF), 0.1),
        'lam_k1': nrm((N_ATT, DH_DIFF), 0.1),
        'lam_q2': nrm((N_ATT, DH_DIFF), 0.1),
        'lam_k2': nrm((N_ATT, DH_DIFF), 0.1),
        'diff_norm_g': 1.0 + nrm((N_ATT, 2 * DH_DIFF), 0.02),
        'w_att_out': nrm((N_ATT, ATT_OUT, d), ATT_OUT ** -0.5 * DEEPNORM_BETA),
        'w_conv_in': nrm((N_CONV, d, CONV_IN), d ** -0.5),
        'conv_w': nrm((N_CONV, CONV_WIDTH, CONV_CH), CONV_WIDTH ** -0.5),
        'conv_b': nrm((N_CONV, CONV_CH), 0.02),
        'conv_norm_g': 1.0 + nrm((N_CONV, CONV_CH), 0.02),
        'conv_norm_b': nrm((N_CONV, CONV_CH), 0.02),
        'w_pool': nrm((N_CONV, POOL_GROUPS, POOL_GC, POOL_GC), POOL_GC ** -0.5),
        'pool_scale': 1.0 + nrm((N_CONV, POOL_CH), 0.1),
        'w_conv_out': nrm((N_CONV, CONV_OUT, d), CONV_OUT ** -0.5 * DEEPNORM_BETA),
    }


def reference(x_prompt, x_sample, cache_diff_k, cache_diff_v, cache_mla_ckv, cache_mla_krope, c, c_ctx,
              w_ada, b_ada, ln_mix_g, ln_mix_b, ln_mlp_g, ln_mlp_b, w_mlp_in, w_mlp_out,
              w_att_in, w_uq, w_ukv, q_norm_g, kv_norm_g, lam_q1, lam_k1, lam_q2, lam_k2, diff_norm_g, w_att_out,
              w_conv_in, conv_w, conv_b, conv_norm_g, conv_norm_b, w_pool, pool_scale, w_conv_out):
    t_lat = x_sample.shape[1]
    rows = t_lat // GRID_W
    row = jnp.repeat(jnp.arange(rows, dtype=jnp.int32), GRID_W)
    col = jnp.tile(jnp.arange(GRID_W, dtype=jnp.int32), rows)

    def mixer(l, h, ctx, pos):
        i = l // 2
        if l % 2 == 0:
            lam_init = 0.8 - 0.6 * math.exp(-0.3 * l)
            lam = (jnp.exp(jnp.sum(lam_q1[i].astype(jnp.float32) * lam_k1[i].astype(jnp.float32)))
                   - jnp.exp(jnp.sum(lam_q2[i].astype(jnp.float32) * lam_k2[i].astype(jnp.float32)))
                   + lam_init)
            return attention_mixer(h, w_att_in[i], w_uq[i], w_ukv[i], q_norm_g[i], kv_norm_g[i], lam, lam_init,
                                   diff_norm_g[i], w_att_out[i], ctx, pos)
        y = conv_pool_mixer(h, w_conv_in[i], conv_w[i], conv_b[i], conv_norm_g[i], conv_norm_b[i],
                            w_pool[i], pool_scale[i], w_conv_out[i])
        return y, None

    def run_layer(l, x, cond, ctx, pos):
        mods = jax.nn.silu(cond) @ w_ada[l] + b_ada[l]
        sh_m, sc_m, g_m, sh_f, sc_f, g_f = jnp.split(mods, 6, axis=-1)
        y, new_ctx = mixer(l, x * (1.0 + sc_m) + sh_m, ctx, pos)
        x = layer_norm(DEEPNORM_ALPHA * x + g_m * y, ln_mix_g[l], ln_mix_b[l])
        f = squared_relu_mlp(x * (1.0 + sc_f) + sh_f, w_mlp_in[l], w_mlp_out[l])
        x = layer_norm(DEEPNORM_ALPHA * x + g_f * f, ln_mlp_g[l], ln_mlp_b[l])
        return x, new_ctx

    xp = x_prompt
    ks_, vs_, ckvs_, kpes_ = [], [], [], []
    for l in range(DEPTH):
        xp, st = run_layer(l, xp, c_ctx, None, None)
        if st is not None:
            ks_.append(st[0])
            vs_.append(st[1])
            ckvs_.append(st[2])
            kpes_.append(st[3])
    new_diff_k = jnp.stack(ks_, axis=1)
    new_diff_v = jnp.stack(vs_, axis=1)
    new_mla_ckv = jnp.stack(ckvs_, axis=1)
    new_mla_krope = jnp.stack(kpes_, axis=1)

    xs = x_sample
    cond = c[:, None, :]
    for l in range(DEPTH):
        i = l // 2
        ctx = (cache_diff_k[:, i], cache_diff_v[:, i], cache_mla_ckv[:, i], cache_mla_krope[:, i]) if l % 2 == 0 else None
        xs, _ = run_layer(l, xs, cond, ctx, (row, col))

    return (xp, xs, new_diff_k, new_diff_v, new_mla_ckv, new_mla_krope)
```

```python
import contextlib
import numpy as np
import concourse.bass as bass
import concourse.mybir as mybir
from concourse.bass_utils import run_bass_kernel_spmd

F32 = mybir.dt.float32
BF16 = mybir.dt.bfloat16
AF = mybir.ActivationFunctionType
ALU = mybir.AluOpType
AX = mybir.AxisListType

NCORES = 8
D = 1024
KC = 8
SEQ = 256
NP = 512
NOWN = 1024
HALO = 16
NQ = NOWN + 2 * HALO
TS = 4096
PAST = 256
NKS = PAST + TS
DFF = 4096
ALPHA = 4.0 ** 0.25
EPS = 1e-5
LAM_INIT0 = 0.8 - 0.6 * 1.0
CONVW = 31
SAME_ENGINE_SYNC = True


class Res:
    __slots__ = ("w", "r", "name", "excl")

    def __init__(self, name="", excl=False):
        self.w = None
        self.r = {}
        self.name = name
        self.excl = excl


class Tracker:
    NS = 8

    def __init__(self, nc, es):
        self.nc = nc
        self.eng = {"pe": nc.tensor, "act": nc.scalar, "dve": nc.vector, "pool": nc.gpsimd, "sp": nc.sync}
        self.sem = {e: es.enter_context(nc.semaphore("s_" + e)) for e in self.eng}
        self.cnt = {e: 0 for e in self.eng}
        self.seen = {e: {} for e in self.eng}
        self.dsem = {q: [es.enter_context(nc.semaphore("d%s%d" % (q, i))) for i in range(self.NS)] for q in ("sp", "pool", "act")}
        self.dcnt = {q: [0] * self.NS for q in self.dsem}
        self.dnext = {q: 0 for q in self.dsem}
        self.out_events = []

    def _wait(self, e, ev):
        key, sh, val = ev[0], ev[1], ev[2]
        if self.seen[e].get(key, 0) >= val:
            return
        self.eng[e].wait_ge(sh, val)
        self.seen[e][key] = val

    @staticmethod
    def _deps(reads, writes):
        evs = []
        for r in reads:
            if r.w is not None:
                evs.append(r.w)
            if r.excl:
                evs.extend(r.r.values())
        for w in writes:
            if w.w is not None:
                evs.append(w.w)
            evs.extend(w.r.values())
        return evs

    def op(self, e, fns, reads=(), writes=(), big=False):
        reads = [r for r in reads if r is not None]
        writes = [w for w in writes if w is not None]
        for ev in self._deps(reads, writes):
            if ev[0] == e and (e == "pe" or not SAME_ENGINE_SYNC or ev[3]):
                continue
            self._wait(e, ev)
        if not isinstance(fns, (list, tuple)):
            fns = [fns]
        ins = None
        for f in fns:
            ins = f()
        self.cnt[e] += 1
        ins.then_inc(self.sem[e], 1)
        ev = (e, self.sem[e], self.cnt[e], big)
        for r in reads:
            r.r[e] = ev
        for w in writes:
            w.w = ev
            w.r = {}

    def dma(self, q, out, in_, reads=(), writes=(), is_output=False):
        reads = [r for r in reads if r is not None]
        writes = [w for w in writes if w is not None]
        for ev in self._deps(reads, writes):
            self._wait(q, ev)
        i = self.dnext[q]
        self.dnext[q] = (i + 1) % self.NS
        key = "d%s%d" % (q, i)
        if self.dcnt[q][i] > 0:
            self._wait(q, (key, self.dsem[q][i], self.dcnt[q][i]))
        self.dcnt[q][i] += 16
        self.eng[q].dma_start(out=out, in_=in_).then_inc(self.dsem[q][i], 16)
        ev = (key, self.dsem[q][i], self.dcnt[q][i], False)
        for r in reads:
            r.r[key] = ev
        for w in writes:
            w.w = ev
            w.r = {}
        if is_output:
            self.out_events.append(ev)

    def barrier(self):
        evs = []
        for q in self.dsem:
            for i in range(self.NS):
                if self.dcnt[q][i] > 0:
                    evs.append(("d%s%d" % (q, i), self.dsem[q][i], self.dcnt[q][i]))
        for e in self.eng:
            if self.cnt[e] > 0:
                evs.append((e, self.sem[e], self.cnt[e]))
        for e in self.eng:
            for ev in evs:
                if ev[0] != e:
                    self._wait(e, ev)

    def finish(self):
        for q in self.dsem:
            for i in range(self.NS):
                if self.dcnt[q][i] > 0:
                    self._wait("sp", ("d%s%d" % (q, i), self.dsem[q][i], self.dcnt[q][i]))
        for e in self.eng:
            if e != "sp" and self.cnt[e] > 0:
                self._wait("sp", (e, self.sem[e], self.cnt[e]))


class _Stop(Exception):
    pass


def build(debug=None):
    try:
        return _build(debug)
    except _Stop as e:
        return e.args[0]


def _build(debug=None):
    nc = bass.Bass("TRN2", target_bir_lowering=False)

    def din(name, shape, dt=F32):
        return nc.dram_tensor(name, list(shape), dt, kind="ExternalInput").ap()

    def dout(name, shape, dt=F32):
        return nc.dram_tensor(name, list(shape), dt, kind="ExternalOutput").ap()

    def dscr(name, shape, dt):
        return nc.dram_tensor(name, list(shape), dt, kind="Internal").ap()

    xp_d = din("xp", [NP, D])
    xq_d = din("xq", [NQ, D])
    xkv_d = din("xkv", [TS, D])
    ck_d = din("ck", [PAST, 512])
    cv_d = din("cv", [PAST, 512])
    cckv_d = din("cckv", [PAST, 128])
    ckr_d = din("ckr", [PAST, 32])
    vecA_d = din("vecA", [22, D])
    vecB_d = din("vecB", [35, 512])
    vecC_d = din("vecC", [3, 256])
    lam_d = din("lamv", [1, 256])
    ident_d = din("ident", [128, 128])
    permd_d = din("permd", [128, 128])
    permpe_d = din("permpe", [32, 32])
    ropek_d = din("ropek", [2, 128, TS])
    ropekpe_d = din("ropekpe", [2, 32, TS])
    ropeq_d = din("ropeq", [2, 128, NQ])
    ropeqpe_d = din("ropeqpe", [2, 32, NQ])
    hmask_d = din("hmask", [1, 32])
    icS_d = din("icS", [4, NOWN])
    icP_d = din("icP", [4, NP])
    w_ada_d = din("w_ada", [2, D, 6 * D])
    w_mlp_in_d = din("w_mlp_in", [2, D, DFF])
    w_mlp_out_d = din("w_mlp_out", [2, DFF, D])
    w_att_in_d = din("w_att_in", [D, 1952])
    w_uq_d = din("w_uq", [256, 768])
    w_ukv_d = din("w_ukv", [128, 1024])
    w_att_out_d = din("w_att_out", [D, D])
    w_conv_in_d = din("w_conv_in", [D, 1536])
    w_pool_d = din("w_pool", [4, 128, 128])
    w_conv_out_d = din("w_conv_out", [D, D])

    yp_d = dout("yp", [NP, D])
    ys_d = dout("ys", [NOWN, D])
    ndk_d = dout("ndk", [NP, 512])
    ndv_d = dout("ndv", [NP, 512])
    nckv_d = dout("nckv", [NP, 128])
    nkr_d = dout("nkr", [NP, 32])

    QTd_P = dscr("QTd_P", [1, 4, 128, NP], BF16)
    KTd_P = dscr("KTd_P", [4, 128, NP], BF16)
    Vd_P = dscr("Vd_P", [NP, 4, 256], BF16)
    QTm_P = dscr("QTm_P", [1, 8, 96, NP], BF16)
    KTm_P = dscr("KTm_P", [8, 96, NP], BF16)
    Vm_P = dscr("Vm_P", [NP, 8, 128], BF16)
    QTd_S = dscr("QTd_S", [2, 4, 128, NQ], BF16)
    KTd_S = dscr("KTd_S", [4, 128, NKS], BF16)
    Vd_S = dscr("Vd_S", [NKS, 4, 256], BF16)
    QTm_S = dscr("QTm_S", [2, 8, 96, NQ], BF16)
    KTm_S = dscr("KTm_S", [8, 96, NKS], BF16)
    Vm_S = dscr("Vm_S", [NKS, 8, 128], BF16)
    cat_P = dscr("cat_P", [NP, D], F32)
    cat_S = dscr("cat_S", [NQ, D], F32)
    dbg = {}

    es = contextlib.ExitStack()
    with es:
        T = Tracker(nc, es)

        uniq = [0]

        def sb(stack, name, shape, dt):
            uniq[0] += 1
            t = stack.enter_context(nc.sbuf_tensor("sb%d_%s" % (uniq[0], name), list(shape), dt))
            return t, Res(name)

        banks = []
        for i in range(8):
            b = es.enter_context(nc.psum_tensor("bank%d" % i, [128, 512], F32))
            banks.append((b, Res("bank%d" % i, excl=True)))
        rot_state = {"i": 0, "set": list(range(8))}

        def nb():
            s = rot_state["set"]
            rot_state["i"] = (rot_state["i"] + 1) % len(s)
            return banks[s[rot_state["i"]]]

        def set_rot(lst):
            rot_state["set"] = list(lst)
            rot_state["i"] = 0

        R = {n: None for n in ["QTd_P", "KTd_P", "Vd_P", "QTm_P", "KTm_P", "Vm_P", "QTd_S", "KTd_S", "Vd_S",
                                 "QTm_S", "KTm_S", "Vm_S", "cat_P", "cat_S"]}
        NOR = Res("const")

        def dump(name, ap, shape, reads):
            d = dout("dbg_" + name, shape, F32)
            T.dma("pool", d, ap, reads=reads, writes=[], is_output=True)

        def stage(name, dumps=()):
            if debug == name:
                for (n_, ap_, shp_, rd_) in dumps:
                    dump(n_, ap_, shp_, rd_)
                T.finish()
                raise _Stop(nc)

        ident, identR = sb(es, "ident", [128, 128], F32)
        identb, identbR = sb(es, "identb", [128, 128], BF16)
        permd, permdR = sb(es, "permd", [128, 128], BF16)
        permpe, permpeR = sb(es, "permpe", [32, 32], BF16)
        onesD, onesDR = sb(es, "onesD", [128, 128], F32)
        onesH, onesHR = sb(es, "onesH", [128, 128], F32)
        vT, vTR = sb(es, "vT", [128, 8, 22], F32)
        vB, vBR = sb(es, "vB", [128, 4, 35], F32)
        vC, vCR = sb(es, "vC", [128, 2, 3], F32)
        mods, modsR = sb(es, "mods", [128, 2, 48, 2], F32)
        cols, colsR = sb(es, "cols", [128, 40, 8], F32)
        lamt, lamtR = sb(es, "lamt", [128, 8], F32)
        silc, silcR = sb(es, "silc", [128, 8, 2], BF16)
        hmask, hmaskR = sb(es, "hmask", [128, 32], F32)
        xrP, xrPR = sb(es, "xrP", [128, 8, NP], F32)
        xrS, xrSR = sb(es, "xrS", [128, 8, NQ], F32)
        hP, hPR = sb(es, "hP", [128, 8, NP], BF16)
        hS, hSR = sb(es, "hS", [128, 8, NQ], BF16)

        T.dma("sp", ident[:], ident_d[:, :], writes=[identR])
        T.dma("pool", identb[:], ident_d[:, :], writes=[identbR])
        T.dma("pool", permd[:], permd_d[:, :], writes=[permdR])
        T.dma("pool", permpe[:], permpe_d[:, :], writes=[permpeR])
        T.dma("sp", hmask[:], hmask_d.to_broadcast([128, 32]), writes=[hmaskR])
        T.op("dve", lambda: nc.vector.memset(onesD[:], 1.0 / D), writes=[onesDR])
        T.op("dve", lambda: nc.vector.memset(onesH[:], 1.0 / 512), writes=[onesHR])

        def load_w(stack, name, src, K, N, q="pool"):
            t, r = sb(stack, name, [128, K, N], BF16)
            sv = src.rearrange("(c p) n -> p c n", p=128)
            step = 1024
            for n0 in range(0, N, step):
                n1 = min(N, n0 + step)
                T.dma(q, t[:, :, n0:n1], sv[:, :, n0:n1], writes=[r])
            return t, r

        xst = [None, None]
        xsti = [0]

        def load_T(src, n, evac):
            nt = (n + 127) // 128
            xt, xtR = xst[xsti[0]]
            xsti[0] ^= 1
            if n % 128 == 0:
                T.dma("pool", xt[:, 0:nt, :], src.rearrange("(t p) f -> p t f", p=128), writes=[xtR])
            else:
                T.dma("pool", xt[0:n, 0, :], src, writes=[xtR])
            for c in range(KC):
                b, bR = nb()
                fns = []
                for t in range(nt):
                    rows = min(128, n - t * 128)
                    fns.append(lambda t=t, rows=rows, b=b, c=c: nc.tensor.transpose(
                        out=b[:, t * 128:t * 128 + rows], in_=xt[0:rows, t, c * 128:(c + 1) * 128],
                        identity=ident[0:rows, 0:rows]))
                T.op("pe", fns, reads=[xtR, identR], writes=[bR])
                evac(c, b, bR)

        def lin_fm(W, WR, c0, nchunk, src, srcR, s0, n, evac, K=KC, m=128):
            pend = None
            for j in range(nchunk):
                b, bR = nb()
                fns = []
                for kc in range(K):
                    fns.append(lambda kc=kc, b=b, j=j: nc.tensor.matmul(
                        b[0:m, 0:n], lhsT=W[:, kc, c0 + j * m:c0 + (j + 1) * m], rhs=src[:, kc, s0:s0 + n],
                        start=(kc == 0), stop=(kc == K - 1)))
                T.op("pe", fns, reads=[WR, srcR], writes=[bR])
                if pend is not None:
                    evac(*pend)
                pend = (j, b, bR)
            if pend is not None:
                evac(*pend)

        def lin_tm(W, WR, c0, ncol, src, srcR, s0, n, evac, K=KC):
            nt = (n + 127) // 128
            pend = None
            for t in range(nt):
                rows = min(128, n - t * 128)
                b, bR = nb()
                fns = []
                for kc in range(K):
                    fns.append(lambda kc=kc, b=b, t=t, rows=rows: nc.tensor.matmul(
                        b[0:rows, 0:ncol], lhsT=src[:, kc, s0 + t * 128:s0 + t * 128 + rows], rhs=W[:, kc, c0:c0 + ncol],
                        start=(kc == 0), stop=(kc == K - 1)))
                T.op("pe", fns, reads=[WR, srcR], writes=[bR])
                if pend is not None:
                    evac(*pend)
                pend = (t, rows, b, bR)
            if pend is not None:
                evac(*pend)

        def mods_dma(l, jb, wt, wtR, nck=4):
            wv = w_ada_d[l].rearrange("(c p) n -> p c n", p=128)
            T.dma("pool", wt[:, :, 0:nck * 128], wv[:, :, jb * nck * 128:(jb + 1) * nck * 128], writes=[wtR])

        def mods_block(l, jb, wt, wtR, nck=4):
            b, bR = nb()
            fns = []
            for jj in range(nck):
                for kc in range(8):
                    fns.append(lambda jj=jj, kc=kc: nc.tensor.matmul(b[:, 2 * jj:2 * jj + 2], lhsT=wt[:, kc, jj * 128:(jj + 1) * 128], rhs=silc[:, kc, :],
                                                                     start=(kc == 0), stop=(kc == 7)))
            T.op("pe", fns, reads=[wtR, silcR], writes=[bR])
            j0 = jb * nck
            sp_ = j0 // 8
            c0 = j0 % 8
            add1 = 1.0 if sp_ in (1, 4) else 0.0
            for cd in range(2):
                T.op("dve", lambda cd=cd: nc.vector.scalar_tensor_tensor(out=mods[:, l, j0:j0 + nck, cd], in0=b[:, cd:2 * nck:2], scalar=add1, in1=vT[:, c0:c0 + nck, l * 6 + sp_],
                                                                        op0=ALU.add, op1=ALU.add), reads=[bR, vTR], writes=[modsR])

        with contextlib.ExitStack() as ps:
            vA, vAR = sb(ps, "vecA", [22, D], F32)
            vBt, vBtR = sb(ps, "vecBt", [35, 512], F32)
            vCt, vCtR = sb(ps, "vecCt", [3, 256], F32)
            lamv, lamvR = sb(ps, "lamv", [128, 256], F32)
            T.dma("sp", vA[:], vecA_d[:, :], writes=[vAR])
            T.dma("sp", vBt[:], vecB_d[:, :], writes=[vBtR])
            T.dma("sp", vCt[:], vecC_d[:, :], writes=[vCtR])
            T.dma("sp", lamv[:], lam_d.to_broadcast([128, 256]), writes=[lamvR])
            for c in range(8):
                b, bR = nb()
                T.op("pe", lambda b=b, c=c: nc.tensor.transpose(out=b[:, 0:22], in_=vA[0:22, c * 128:(c + 1) * 128], identity=ident[0:22, 0:22]),
                     reads=[vAR, identR], writes=[bR])
                T.op("dve", lambda b=b, c=c: nc.vector.tensor_copy(out=vT[:, c, :], in_=b[:, 0:22]), reads=[bR], writes=[vTR])
            for c in range(4):
                b, bR = nb()
                T.op("pe", lambda b=b, c=c: nc.tensor.transpose(out=b[:, 0:35], in_=vBt[0:35, c * 128:(c + 1) * 128], identity=ident[0:35, 0:35]),
                     reads=[vBtR, identR], writes=[bR])
                T.op("dve", lambda b=b, c=c: nc.vector.tensor_copy(out=vB[:, c, :], in_=b[:, 0:35]), reads=[bR], writes=[vBR])
            for c in range(2):
                b, bR = nb()
                T.op("pe", lambda b=b, c=c: nc.tensor.transpose(out=b[:, 0:3], in_=vCt[0:3, c * 128:(c + 1) * 128], identity=ident[0:3, 0:3]),
                     reads=[vCtR, identR], writes=[bR])
                T.op("dve", lambda b=b, c=c: nc.vector.tensor_copy(out=vC[:, c, :], in_=b[:, 0:3]), reads=[bR], writes=[vCR])
            T.op("dve", lambda: nc.vector.tensor_tensor(out=lamv[:, 0:64], in0=lamv[:, 0:64], in1=lamv[:, 64:128], op=ALU.mult), reads=[lamvR], writes=[lamvR])
            T.op("dve", lambda: nc.vector.tensor_tensor(out=lamv[:, 128:192], in0=lamv[:, 128:192], in1=lamv[:, 192:256], op=ALU.mult), reads=[lamvR], writes=[lamvR])
            T.op("dve", lambda: nc.vector.reduce_sum(out=lamt[:, 0:1], in_=lamv[:, 0:64], axis=AX.X), reads=[lamvR], writes=[lamtR])
            T.op("dve", lambda: nc.vector.reduce_sum(out=lamt[:, 1:2], in_=lamv[:, 128:192], axis=AX.X), reads=[lamvR], writes=[lamtR])
            T.op("act", lambda: nc.scalar.activation(out=lamt[:, 2:4], in_=lamt[:, 0:2], func=AF.Exp), reads=[lamtR], writes=[lamtR])
            T.op("dve", lambda: nc.vector.tensor_tensor(out=lamt[:, 4:5], in0=lamt[:, 3:4], in1=lamt[:, 2:3], op=ALU.subtract), reads=[lamtR], writes=[lamtR])
            T.op("dve", lambda: nc.vector.tensor_scalar(out=lamt[:, 4:5], in0=lamt[:, 4:5], scalar1=-LAM_INIT0, scalar2=None, op0=ALU.add), reads=[lamtR], writes=[lamtR])
            T.op("act", lambda: nc.scalar.activation(out=silc[:, :, 0], in_=vT[:, :, 21], func=AF.Silu), reads=[vTR], writes=[silcR])
            T.op("act", lambda: nc.scalar.activation(out=silc[:, :, 1], in_=vT[:, :, 20], func=AF.Silu), reads=[vTR], writes=[silcR])
            wab = [sb(ps, "wab%d" % i, [128, 8, 512], BF16) for i in range(3)]
            for jb in range(12):
                wt, wtR = wab[jb % 3]
                mods_dma(0, jb, wt, wtR)
                mods_block(0, jb, wt, wtR)

        T.barrier()
        def CI(l, cd, k):
            return (l * 2 + cd) * 8 + k

        def M(l, s, cd):
            return mods[:, l, s * 8:(s + 1) * 8, cd]

        def derive_cols(l):
            for cd in range(2):
                g1 = vT[:, :, 12 + 4 * l]
                b1 = vT[:, :, 13 + 4 * l]
                g2 = vT[:, :, 14 + 4 * l]
                b2 = vT[:, :, 15 + 4 * l]

                def tt(o, a, b_, op):
                    T.op("dve", lambda: nc.vector.tensor_tensor(out=o, in0=a, in1=b_, op=op), reads=[modsR, vTR, colsR], writes=[colsR])

                def tsc(o, a, s1):
                    T.op("dve", lambda: nc.vector.tensor_scalar(out=o, in0=a, scalar1=s1, scalar2=None, op0=ALU.mult), reads=[vTR, colsR], writes=[colsR])
                tt(cols[:, CI(l, cd, 0), :], g1, M(l, 4, cd), ALU.mult)
                tt(cols[:, CI(l, cd, 1), :], b1, M(l, 4, cd), ALU.mult)
                tt(cols[:, CI(l, cd, 1), :], cols[:, CI(l, cd, 1), :], M(l, 3, cd), ALU.add)
                if l == 0:
                    tsc(cols[:, CI(l, cd, 6), :], g2, ALPHA)
                    tsc(cols[:, CI(l, cd, 7), :], b2, ALPHA)
                else:
                    g20 = vT[:, :, 14]
                    b20 = vT[:, :, 15]
                    tt(cols[:, CI(0, cd, 2), :], g20, M(1, 1, cd), ALU.mult)
                    tt(cols[:, CI(0, cd, 3), :], b20, M(1, 1, cd), ALU.mult)
                    tt(cols[:, CI(0, cd, 3), :], cols[:, CI(0, cd, 3), :], M(1, 0, cd), ALU.add)
                    tsc(cols[:, CI(l, cd, 6), :], g2, 1.0)
                    tsc(cols[:, CI(l, cd, 7), :], b2, 1.0)
                tsc(cols[:, CI(l, cd, 4), :], g1, ALPHA)
                tsc(cols[:, CI(l, cd, 5), :], b1, ALPHA)
        derive_cols(0)
        T.op("dve", lambda: nc.vector.tensor_scalar(out=cols[:, 32, 0:1], in0=vC[:, 0, 2:3], scalar1=(1.0 - LAM_INIT0), scalar2=None, op0=ALU.mult), reads=[vCR], writes=[colsR])
        T.op("dve", lambda: nc.vector.memset(cols[:, 33, :], 1.0), writes=[colsR])

        stage("prologue", [("mods", mods[:].rearrange("p a b c -> p (a b c)"), [128, 192], [modsR]), ("cols", cols[:].rearrange("p a b -> p (a b)"), [128, 320], [colsR]),
                           ("lamt", lamt[:], [128, 8], [lamtR]), ("vT", vT[:].rearrange("p a b -> p (a b)"), [128, 176], [vTR])])

        def ln_fm(stack_tmps, z, zR, C, s0, n, ones, onesR, outs):
            sq, sqR, mean, meanR, rstd, rstdR, tA, tAR = stack_tmps
            bm, bmR = nb()
            bq, bqR = nb()
            F32R = mybir.dt.float32r
            fns = [lambda c=c: nc.tensor.matmul(bm[:, 0:n], lhsT=ones[:], rhs=z[:, c, s0:s0 + n], start=(c == 0), stop=(c == C - 1)) for c in range(C)]
            T.op("pe", fns, reads=[zR, onesR], writes=[bmR])
            for c in range(C):
                T.op("act", lambda c=c: nc.scalar.activation(out=sq[c % 2][:, 0:n], in_=z[:, c, s0:s0 + n], func=AF.Square), reads=[zR], writes=[sqR[c % 2]], big=(n >= 256))
                T.op("pe", lambda c=c: nc.tensor.matmul(bq[:, 0:n], lhsT=ones[:], rhs=sq[c % 2][:, 0:n], start=(c == 0), stop=(c == C - 1)),
                     reads=[sqR[c % 2], onesR], writes=[bqR])
            T.op("act", lambda: nc.scalar.copy(out=mean[:, 0:n], in_=bm[:, 0:n]), reads=[bmR], writes=[meanR], big=(n >= 256))
            T.op("dve", lambda: nc.vector.tensor_tensor(out=rstd[:, 0:n], in0=mean[:, 0:n], in1=mean[:, 0:n], op=ALU.mult), reads=[meanR], writes=[rstdR], big=(n >= 256))
            T.op("dve", lambda: nc.vector.tensor_tensor(out=rstd[:, 0:n], in0=bq[:, 0:n], in1=rstd[:, 0:n], op=ALU.subtract), reads=[bqR, rstdR], writes=[rstdR], big=(n >= 256))
            T.op("dve", lambda: nc.vector.tensor_scalar(out=rstd[:, 0:n], in0=rstd[:, 0:n], scalar1=EPS, scalar2=None, op0=ALU.add), reads=[rstdR], writes=[rstdR], big=(n >= 256))
            T.op("act", lambda: nc.scalar.activation(out=rstd[:, 0:n], in_=rstd[:, 0:n], func=AF.Ln), reads=[rstdR], writes=[rstdR], big=(n >= 256))
            T.op("act", lambda: nc.scalar.activation(out=rstd[:, 0:n], in_=rstd[:, 0:n], func=AF.Exp, scale=-0.5), reads=[rstdR], writes=[rstdR], big=(n >= 256))
            for c in range(C):
                ta, taR = tA[c % 2], tAR[c % 2]
                T.op("dve", lambda c=c, ta=ta: nc.vector.tensor_tensor(out=ta[:, 0:n], in0=z[:, c, s0:s0 + n], in1=mean[:, 0:n], op=ALU.subtract),
                     reads=[zR, meanR], writes=[taR], big=(n >= 256))
                T.op("dve", lambda ta=ta: nc.vector.tensor_tensor(out=ta[:, 0:n], in0=ta[:, 0:n], in1=rstd[:, 0:n], op=ALU.mult),
                     reads=[rstdR, taR], writes=[taR], big=(n >= 256))
                for (eng, fn, rds, wrs) in outs:
                    T.op(eng, lambda fn=fn, c=c, ta=ta: fn(c, ta[:, 0:n]), reads=[taR] + rds, writes=wrs, big=(n >= 256))

        def act_affine(out_ap, in_ap, scale_col, bias_col, func=AF.Identity):
            return nc.scalar.activation(out=out_ap, in_=in_ap, func=func, scale=scale_col, bias=bias_col)

        def dve_affine(out_ap, in_ap, scale_col, bias_col):
            return nc.vector.tensor_scalar(out=out_ap, in0=in_ap, scalar1=scale_col, scalar2=bias_col, op0=ALU.mult, op1=ALU.add)

        with contextlib.ExitStack() as l0:
            lnt = None
            with contextlib.ExitStack() as pj:
                xst[0] = sb(pj, "xst0", [128, 4, D], F32)
                xst[1] = sb(pj, "xst1", [128, 4, D], F32)
                wA, wAR = load_w(pj, "wA", w_att_in_d, 8, 1952)
                wuq, wuqR = load_w(pj, "wuq", w_uq_d, 2, 768)
                wukv, wukvR = load_w(pj, "wukv", w_ukv_d, 1, 1024)
                hk = [sb(pj, "hk%d" % i, [128, 8, 512], BF16) for i in range(2)]
                rk = [sb(pj, "rk%d" % i, [128, 2, 512], F32) for i in range(2)]
                rp = [sb(pj, "rp%d" % i, [32, 2, 512], F32) for i in range(1)]
                kraw = [sb(pj, "kraw%d" % i, [128, 512], BF16) for i in range(1)]
                t1 = [sb(pj, "t1_%d" % i, [128, 512], F32) for i in range(1)]
                t2 = [sb(pj, "t2_%d" % i, [128, 512], F32) for i in range(1)]
                ko = [sb(pj, "ko%d" % i, [128, 512], BF16) for i in range(2)]
                vo = [sb(pj, "vo%d" % i, [128, 8, 128], BF16) for i in range(2)]
                vmo = [sb(pj, "vmo%d" % i, [128, 8, 128], BF16) for i in range(2)]
                of32 = [sb(pj, "of32_%d" % i, [128, 512], F32) for i in range(1)]
                ckv_t = [sb(pj, "ckvt%d" % i, [128, 160], F32) for i in range(2)]
                ckv_all = sb(pj, "ckv_all", [128, 4, 160], F32)
                sq_all = sb(pj, "sq_all", [128, 4, 256], F32)
                cq_all = sb(pj, "cq_all", [128, 4, 256], F32)
                small = [sb(pj, "small%d" % i, [128, 8], F32) for i in range(2)]
                cqn, cqnR = sb(pj, "cqn", [128, 2, 512], BF16)
                ckvn, ckvnR = sb(pj, "ckvn", [128, 1, 512], BF16)
                kvg, kvgR = sb(pj, "kvg", [128, 128], F32)
                T.dma("sp", kvg[:], vecC_d[1:2, 0:128].to_broadcast([128, 128]), writes=[kvgR])
                for i in range(2):
                    T.op("dve", lambda i=i: nc.vector.memset(vo[i][0][:, :, 64:128], 1.0), writes=[vo[i][1]])
                    T.op("dve", lambda i=i: nc.vector.memset(vmo[i][0][:, :, 64:128], 1.0), writes=[vmo[i][1]])
                cnt = {"k": 0}

                def rr(lst):
                    k = cnt.get(id(lst), 0) + 1
                    cnt[id(lst)] = k
                    return lst[k % len(lst)]

                def rope_fm(b, bR, n, rows, perm, permR, tab, tabR, dst, dstR):
                    kr, krR = rr(kraw)
                    T.op("act", lambda: nc.scalar.copy(out=kr[0:rows, 0:n], in_=b[0:rows, 0:n]), reads=[bR], writes=[krR], big=(n >= 256))
                    b2, b2R = nb()
                    T.op("pe", lambda: nc.tensor.matmul(b2[0:rows, 0:n], lhsT=perm[0:rows, 0:rows], rhs=kr[0:rows, 0:n], start=True, stop=True),
                         reads=[krR, permR], writes=[b2R])
                    a1, a1R = rr(t1)
                    a2, a2R = rr(t2)
                    T.op("dve", lambda: nc.vector.tensor_tensor(out=a1[0:rows, 0:n], in0=b[0:rows, 0:n], in1=tab[0:rows, 0, 0:n], op=ALU.mult), reads=[bR, tabR], writes=[a1R], big=(n >= 256))
                    T.op("dve", lambda: nc.vector.tensor_tensor(out=a2[0:rows, 0:n], in0=b2[0:rows, 0:n], in1=tab[0:rows, 1, 0:n], op=ALU.mult), reads=[b2R, tabR], writes=[a2R], big=(n >= 256))
                    T.op("dve", lambda: nc.vector.tensor_tensor(out=dst[0:rows, 0:n], in0=a1[0:rows, 0:n], in1=a2[0:rows, 0:n], op=ALU.add), reads=[a1R, a2R], writes=[dstR], big=(n >= 256))

                def rstd_cols(ss_ap, n_feat, rows, sm, smR, col):
                    T.op("dve", lambda: nc.vector.tensor_scalar(out=sm[0:rows, col:col + 1], in0=ss_ap, scalar1=1.0 / n_feat, scalar2=EPS, op0=ALU.mult, op1=ALU.add), reads=[smR], writes=[smR])
                    T.op("act", lambda: nc.scalar.activation(out=sm[0:rows, col:col + 1], in_=sm[0:rows, col:col + 1], func=AF.Ln), reads=[smR], writes=[smR])
                    T.op("act", lambda: nc.scalar.activation(out=sm[0:rows, col:col + 1], in_=sm[0:rows, col:col + 1], func=AF.Exp, scale=-0.5), reads=[smR], writes=[smR])

                def kv_group(h, hR, s0, n, kc0, is_prompt, tab, tabR, tabp, tabpR, KTd, KTm, Vd, Vm, rKTd, rKTm, rVd, rVm, tok0):
                    nt_ = (n + 127) // 128
                    rw = min(128, n)
                    cta, ctaR = ckv_all

                    def ev_ckv(t, rows, b, bR):
                        T.op("act", lambda: nc.scalar.copy(out=cta[0:rows, t, 0:160], in_=b[0:rows, 0:160]), reads=[bR], writes=[ctaR])

                    def post_ckv():
                        sm, smR = rr(small)
                        a1, a1R = sq_all
                        T.op("dve", lambda: nc.vector.tensor_tensor(out=a1[0:rw, 0:nt_, 0:128], in0=cta[0:rw, 0:nt_, 0:128], in1=cta[0:rw, 0:nt_, 0:128], op=ALU.mult), reads=[ctaR], writes=[a1R], big=True)
                        T.op("dve", lambda: nc.vector.reduce_sum(out=sm[0:rw, 0:nt_], in_=a1[0:rw, 0:nt_, 0:128], axis=AX.X), reads=[a1R], writes=[smR])
                        T.op("dve", lambda: nc.vector.tensor_scalar(out=sm[0:rw, 0:nt_], in0=sm[0:rw, 0:nt_], scalar1=1.0 / 128, scalar2=EPS, op0=ALU.mult, op1=ALU.add), reads=[smR], writes=[smR])
                        T.op("act", lambda: nc.scalar.activation(out=sm[0:rw, 0:nt_], in_=sm[0:rw, 0:nt_], func=AF.Ln), reads=[smR], writes=[smR])
                        T.op("act", lambda: nc.scalar.activation(out=sm[0:rw, 0:nt_], in_=sm[0:rw, 0:nt_], func=AF.Exp, scale=-0.5), reads=[smR], writes=[smR])
                        for t in range(nt_):
                            T.op("dve", lambda t=t: nc.vector.tensor_scalar(out=cta[0:rw, t, 0:128], in0=cta[0:rw, t, 0:128], scalar1=sm[0:rw, t:t + 1], scalar2=None, op0=ALU.mult), reads=[smR, ctaR], writes=[ctaR], big=True)
                        if is_prompt:
                            for t in range(nt_):
                                f, fR = rr(of32)
                                T.op("dve", lambda t=t, f=f: nc.vector.tensor_tensor(out=f[0:rw, 0:128], in0=cta[0:rw, t, 0:128], in1=kvg[0:rw, :], op=ALU.mult), reads=[ctaR, kvgR], writes=[fR], big=True)
                                T.dma("sp", nckv_d[tok0 + t * 128:tok0 + t * 128 + rw, :], f[0:rw, 0:128], reads=[fR], writes=[], is_output=True)
                                T.dma("sp", nkr_d[tok0 + t * 128:tok0 + t * 128 + rw, :], cta[0:rw, t, 128:160], reads=[ctaR], writes=[], is_output=True)

                    def post_ckv2():
                        b2, b2R = nb()
                        fns = [lambda t=t: nc.tensor.transpose(out=b2[:, t * 128:t * 128 + rw], in_=cta[0:rw, t, 0:128], identity=ident[0:rw, 0:rw]) for t in range(nt_)]
                        T.op("pe", fns, reads=[ctaR, identR], writes=[b2R])
                        T.op("act", lambda: nc.scalar.activation(out=ckvn[:, 0, 0:n], in_=b2[:, 0:n], func=AF.Copy, scale=vC[:, 0, 1:2]), reads=[b2R, vCR], writes=[ckvnR], big=True)
                    lin_tm(wA, wAR, 1792, 160, h, hR, s0, n, ev_ckv)
                    post_ckv()
                    def ev_k(j, b, bR):
                        o, oR = rr(ko)
                        if is_prompt:
                            T.op("act", lambda: nc.scalar.copy(out=o[:, 0:n], in_=b[:, 0:n]), reads=[bR], writes=[oR])
                        else:
                            rope_fm(b, bR, n, 128, permd, permdR, tab, tabR, o, oR)
                        T.dma("sp", KTd[j, :, kc0:kc0 + n], o[:, 0:n], reads=[oR], writes=[rKTd])
                    if is_prompt:
                        stage("p1", [("xrP", xrP[:].rearrange("p a b -> p (a b)"), [128, 8 * NP], [xrPR]), ("hP", hP[:].rearrange("p a b -> p (a b)"), [128, 8 * NP], [hPR]),
                                     ("wA", wA[:, 0, 0:512], [128, 512], [wAR])])
                    lin_fm(wA, wAR, 512, 4, h, hR, s0, n, ev_k)
                    if is_prompt:
                        stage("p2")
                    kp, kpR = rr(ko)

                    def ev_kp(j, b, bR):
                        if is_prompt:
                            T.op("act", lambda: nc.scalar.copy(out=kp[0:32, 0:n], in_=b[0:32, 0:n]), reads=[bR], writes=[kpR])
                        else:
                            rope_fm(b, bR, n, 32, permpe, permpeR, tabp, tabpR, kp, kpR)
                        for hh in range(8):
                            T.dma("sp", KTm[hh, 64:96, kc0:kc0 + n], kp[0:32, 0:n], reads=[kpR], writes=[rKTm])
                    lin_fm(wA, wAR, 1920, 1, h, hR, s0, n, ev_kp, m=32)
                    if is_prompt:
                        stage("p3")
                    def ev_v(t, rows, b, bR):
                        o, oR = rr(vo)
                        T.op("act", lambda: nc.scalar.copy(out=o[0:rows, :, 0:64], in_=b[0:rows, 0:512].rearrange("p (h e) -> p h e", h=8)), reads=[bR], writes=[oR])
                        T.dma("sp", Vd[kc0 + t * 128:kc0 + t * 128 + rows, :, :].rearrange("k h (t e) -> k (h t) e", t=2), o[0:rows, :, :], reads=[oR], writes=[rVd])
                        if is_prompt:
                            f, fR = rr(of32)
                            T.op("dve", lambda: nc.vector.tensor_copy(out=f[0:rows, 0:512], in_=b[0:rows, 0:512]), reads=[bR], writes=[fR])
                            T.dma("sp", ndv_d[tok0 + t * 128:tok0 + t * 128 + rows, :], f[0:rows, 0:512], reads=[fR], writes=[], is_output=True)
                    lin_tm(wA, wAR, 1024, 512, h, hR, s0, n, ev_v)
                    if is_prompt:
                        stage("p3a")
                    if is_prompt:
                        def ev_kout(t, rows, b, bR):
                            f, fR = rr(of32)
                            T.op("dve", lambda: nc.vector.tensor_copy(out=f[0:rows, 0:512], in_=b[0:rows, 0:512]), reads=[bR], writes=[fR])
                            T.dma("sp", ndk_d[tok0 + t * 128:tok0 + t * 128 + rows, :], f[0:rows, 0:512], reads=[fR], writes=[], is_output=True)
                        lin_tm(wA, wAR, 512, 512, h, hR, s0, n, ev_kout)

                    post_ckv2()
                    kv_from_ckvn(ckvn, ckvnR, n, kc0, KTm, Vm, rKTm, rVm)

                def kv_from_ckvn(cn, cnR, n, kc0, KTm, Vm, rKTm, rVm):
                    def ev_kn(j, b, bR):
                        o, oR = rr(ko)
                        T.op("act", lambda: nc.scalar.copy(out=o[:, 0:n], in_=b[:, 0:n]), reads=[bR], writes=[oR])
                        T.dma("sp", KTm[j, 0:64, kc0:kc0 + n], o[0:64, 0:n], reads=[oR], writes=[rKTm])
                    lin_fm(wukv, wukvR, 0, 8, cn, cnR, 0, n, ev_kn, K=1)

                    def ev_vm(t, rows, b, bR):
                        pass
                    nt = (n + 127) // 128
                    for t in range(nt):
                        rows = min(128, n - t * 128)
                        o, oR = rr(vmo)
                        for half in range(2):
                            b, bR = nb()
                            T.op("pe", lambda b=b, half=half: nc.tensor.matmul(b[0:rows, 0:512], lhsT=cn[:, 0, t * 128:t * 128 + rows], rhs=wukv[:, 0, half * 512:(half + 1) * 512], start=True, stop=True),
                                 reads=[cnR, wukvR], writes=[bR])
                            T.op("act", lambda b=b, half=half: nc.scalar.copy(out=o[0:rows, half * 4:(half + 1) * 4, 0:64],
                                                                             in_=b[0:rows, 0:512].rearrange("p (h e) -> p h e", h=4)[:, :, 64:128]), reads=[bR], writes=[oR])
                        T.dma("sp", Vm[kc0 + t * 128:kc0 + t * 128 + rows, :, :], o[0:rows, :, :], reads=[oR], writes=[rVm])

                def q_group(h, hR, s0, n, is_prompt, tab, tabR, tabp, tabpR, QTd, QTm, rQTd, rQTm, qc0):
                    def ev_q(j, b, bR):
                        o, oR = rr(ko)
                        T.op("act", lambda: nc.scalar.copy(out=o[:, 0:n], in_=b[:, 0:n]), reads=[bR], writes=[oR])
                        T.dma("sp", QTd[0, j, :, qc0:qc0 + n], o[:, 0:n], reads=[oR], writes=[rQTd])
                        if not is_prompt:
                            o2, o2R = rr(ko)
                            rope_fm(b, bR, n, 128, permd, permdR, tab, tabR, o2, o2R)
                            T.dma("sp", QTd[1, j, :, qc0:qc0 + n], o2[:, 0:n], reads=[o2R], writes=[rQTd])
                    lin_fm(wA, wAR, 0, 4, h, hR, s0, n, ev_q)

                    nt_ = (n + 127) // 128
                    rw = min(128, n)
                    cqa, cqaR = cq_all

                    def ev_cq(t, rows, b, bR):
                        T.op("act", lambda: nc.scalar.copy(out=cqa[0:rows, t, :], in_=b[0:rows, 0:256]), reads=[bR], writes=[cqaR])

                    def post_cq():
                        sm, smR = rr(small)
                        a1, a1R = sq_all
                        T.op("dve", lambda: nc.vector.tensor_tensor(out=a1[0:rw, 0:nt_, :], in0=cqa[0:rw, 0:nt_, :], in1=cqa[0:rw, 0:nt_, :], op=ALU.mult), reads=[cqaR], writes=[a1R], big=True)
                        T.op("dve", lambda: nc.vector.reduce_sum(out=sm[0:rw, 0:nt_], in_=a1[0:rw, 0:nt_, :], axis=AX.X), reads=[a1R], writes=[smR])
                        T.op("dve", lambda: nc.vector.tensor_scalar(out=sm[0:rw, 0:nt_], in0=sm[0:rw, 0:nt_], scalar1=1.0 / 256, scalar2=EPS, op0=ALU.mult, op1=ALU.add), reads=[smR], writes=[smR])
                        T.op("act", lambda: nc.scalar.activation(out=sm[0:rw, 0:nt_], in_=sm[0:rw, 0:nt_], func=AF.Ln), reads=[smR], writes=[smR])
                        T.op("act", lambda: nc.scalar.activation(out=sm[0:rw, 0:nt_], in_=sm[0:rw, 0:nt_], func=AF.Exp, scale=-0.5), reads=[smR], writes=[smR])
                        for t in range(nt_):
                            T.op("dve", lambda t=t: nc.vector.tensor_scalar(out=cqa[0:rw, t, :], in0=cqa[0:rw, t, :], scalar1=sm[0:rw, t:t + 1], scalar2=None, op0=ALU.mult), reads=[smR, cqaR], writes=[cqaR], big=True)
                        for c2 in range(2):
                            b2, b2R = nb()
                            fns = [lambda t=t, c2=c2, b2=b2: nc.tensor.transpose(out=b2[:, t * 128:t * 128 + rw], in_=cqa[0:rw, t, c2 * 128:(c2 + 1) * 128], identity=ident[0:rw, 0:rw]) for t in range(nt_)]
                            T.op("pe", fns, reads=[cqaR, identR], writes=[b2R])
                            T.op("act", lambda b2=b2, c2=c2: nc.scalar.activation(out=cqn[:, c2, 0:n], in_=b2[:, 0:n], func=AF.Copy, scale=vC[:, c2, 0:1]), reads=[b2R, vCR], writes=[cqnR], big=True)
                    lin_tm(wA, wAR, 1536, 256, h, hR, s0, n, ev_cq)
                    post_cq()
                    for hh in range(8):
                        def ev_qn(j, b, bR, hh=hh):
                            o, oR = rr(ko)
                            T.op("act", lambda: nc.scalar.copy(out=o[0:64, 0:n], in_=b[0:64, 0:n]), reads=[bR], writes=[oR])
                            for v in range(1 if is_prompt else 2):
                                T.dma("sp", QTm[v, hh, 0:64, qc0:qc0 + n], o[0:64, 0:n], reads=[oR], writes=[rQTm])
                        lin_fm(wuq, wuqR, hh * 96, 1, cqn, cqnR, 0, n, ev_qn, K=2, m=64)

                        def ev_qp(j, b, bR, hh=hh):
                            o, oR = rr(ko)
                            T.op("act", lambda: nc.scalar.copy(out=o[0:32, 0:n], in_=b[0:32, 0:n]), reads=[bR], writes=[oR])
                            T.dma("sp", QTm[0, hh, 64:96, qc0:qc0 + n], o[0:32, 0:n], reads=[oR], writes=[rQTm])
                            if not is_prompt:
                                o2, o2R = rr(ko)
                                rope_fm(b, bR, n, 32, permpe, permpeR, tabp, tabpR, o2, o2R)
                                T.dma("sp", QTm[1, hh, 64:96, qc0:qc0 + n], o2[0:32, 0:n], reads=[o2R], writes=[rQTm])
                        lin_fm(wuq, wuqR, hh * 96 + 64, 1, cqn, cqnR, 0, n, ev_qp, K=2, m=32)

                def mod_evac(hdst, hdstR, xr, xrR, s0, n, l, cd):
                    def ev(c, b, bR):
                        T.op("act", lambda: act_affine(hdst[:, c, s0:s0 + n], b[:, 0:n], mods[:, l, 8 + c:9 + c, cd], mods[:, l, c:c + 1, cd]),
                             reads=[bR, modsR], writes=[hdstR], big=True)
                        if xr is not None:
                            T.op("dve", lambda: nc.vector.tensor_scalar(out=xr[:, c, s0:s0 + n], in0=b[:, 0:n], scalar1=ALPHA, scalar2=None, op0=ALU.mult),
                                 reads=[bR], writes=[xrR], big=True)
                    return ev

                load_T(xp_d[:, :], NP, mod_evac(hP, hPR, xrP, xrPR, 0, NP, 0, 0))
                kv_group(hP, hPR, 0, NP, 0, True, None, None, None, None, KTd_P, KTm_P, Vd_P, Vm_P, R["KTd_P"], R["KTm_P"], R["Vd_P"], R["Vm_P"], 0)
                q_group(hP, hPR, 0, NP, True, None, None, None, None, QTd_P, QTm_P, R["QTd_P"], R["QTm_P"], 0)

                stage("projP", [("hP", None, None, None)] if False else [])
                for t in range(2):
                    ct, ctR = rr(of32)
                    T.dma("sp", ct[:, 0:512], ck_d[t * 128:(t + 1) * 128, :], writes=[ctR])
                    for hh in range(4):
                        b, bR = nb()
                        T.op("pe", lambda b=b, hh=hh, ct=ct: nc.tensor.transpose(out=b[:, 0:128], in_=ct[:, hh * 128:(hh + 1) * 128], identity=ident[:, :]), reads=[ctR, identR], writes=[bR])
                        o, oR = rr(ko)
                        T.op("act", lambda b=b, o=o: nc.scalar.copy(out=o[:, 0:128], in_=b[:, 0:128]), reads=[bR], writes=[oR])
                        T.dma("sp", KTd_S[hh, :, t * 128:(t + 1) * 128], o[:, 0:128], reads=[oR], writes=[R["KTd_S"]])
                    cvt, cvtR = rr(of32)
                    T.dma("sp", cvt[:, 0:512], cv_d[t * 128:(t + 1) * 128, :], writes=[cvtR])
                    o, oR = rr(vo)
                    T.op("act", lambda o=o, cvt=cvt: nc.scalar.copy(out=o[:, :, 0:64], in_=cvt[:, 0:512].rearrange("p (h e) -> p h e", h=8)), reads=[cvtR], writes=[oR])
                    T.dma("sp", Vd_S[t * 128:(t + 1) * 128, :, :].rearrange("k h (t e) -> k (h t) e", t=2), o[:, :, :], reads=[oR], writes=[R["Vd_S"]])
                    c2, c2R = rr(ckv_t)
                    T.dma("sp", c2[:, 0:128], cckv_d[t * 128:(t + 1) * 128, :], writes=[c2R])
                    T.dma("sp", c2[:, 128:160], ckr_d[t * 128:(t + 1) * 128, :], writes=[c2R])
                    b, bR = nb()
                    T.op("pe", lambda b=b, c2=c2: nc.tensor.transpose(out=b[:, 0:128], in_=c2[:, 0:128], identity=ident[:, :]), reads=[c2R, identR], writes=[bR])
                    T.op("act", lambda b=b, t=t: nc.scalar.copy(out=ckvn[:, 0, t * 128:(t + 1) * 128], in_=b[:, 0:128]), reads=[bR], writes=[ckvnR])
                    b, bR = nb()
                    T.op("pe", lambda b=b, c2=c2: nc.tensor.transpose(out=b[0:32, 0:128], in_=c2[:, 128:160], identity=ident[:, :]), reads=[c2R, identR], writes=[bR])
                    o, oR = rr(ko)
                    T.op("act", lambda b=b, o=o: nc.scalar.copy(out=o[0:32, 0:128], in_=b[0:32, 0:128]), reads=[bR], writes=[oR])
                    for hh in range(8):
                        T.dma("sp", KTm_S[hh, 64:96, t * 128:(t + 1) * 128], o[0:32, 0:128], reads=[oR], writes=[R["KTm_S"]])
                kv_from_ckvn(ckvn, ckvnR, 256, 0, KTm_S, Vm_S, R["KTm_S"], R["Vm_S"])

                for g in range(TS // 512):
                    hh_, hhR = hk[g % 2]
                    tab, tabR = rk[g % 2]
                    tabp, tabpR = rp[0]
                    T.dma("pool", tab[:, :, :], ropek_d[:, :, g * 512:(g + 1) * 512].rearrange("a p t -> p a t"), writes=[tabR])
                    T.dma("pool", tabp[:, :, :], ropekpe_d[:, :, g * 512:(g + 1) * 512].rearrange("a p t -> p a t"), writes=[tabpR])
                    load_T(xkv_d[g * 512:(g + 1) * 512, :], 512, mod_evac(hh_, hhR, None, None, 0, 512, 0, 1))
                    kv_group(hh_, hhR, 0, 512, PAST + g * 512, False, tab, tabR, tabp, tabpR, KTd_S, KTm_S, Vd_S, Vm_S,
                             R["KTd_S"], R["KTm_S"], R["Vd_S"], R["Vm_S"], 0)
                for (s0, n) in ((0, 512), (512, 512), (1024, 32)):
                    tab, tabR = rr(rk)
                    tabp, tabpR = rr(rp)
                    T.dma("pool", tab[:, :, 0:n], ropeq_d[:, :, s0:s0 + n].rearrange("a p t -> p a t"), writes=[tabR])
                    T.dma("pool", tabp[:, :, 0:n], ropeqpe_d[:, :, s0:s0 + n].rearrange("a p t -> p a t"), writes=[tabpR])
                    load_T(xq_d[s0:s0 + n, :], n, mod_evac(hS, hSR, xrS, xrSR, s0, n, 0, 1))
                    q_group(hS, hSR, s0, n, False, tab, tabR, tabp, tabpR, QTd_S, QTm_S, R["QTd_S"], R["QTm_S"], s0)

            T.barrier()
            stage("proj")
            with contextlib.ExitStack() as at:
                Kb = [sb(at, "Kb%d" % i, [128, NKS], BF16) for i in range(2)]
                Vb = [sb(at, "Vb%d" % i, [128, 34, 256], BF16) for i in range(2)]
                Qb = [sb(at, "Qb%d" % i, [128, 2, 2, NQ], BF16) for i in range(2)]
                for i in range(2):
                    T.op("dve", lambda i=i: nc.vector.memset(Qb[i][0][:, :, :, :], 0.0), writes=[Qb[i][1]])
                Qm = [sb(at, "Qm%d" % i, [128, 2, NQ], BF16) for i in range(2)]
                PT = [sb(at, "PT%d" % i, [128, 512], BF16) for i in range(4)]
                ofm = [sb(at, "ofm%d" % i, [128, 2, 512], F32) for i in range(2)]
                otm = [sb(at, "otm%d" % i, [128, 4, 256], F32) for i in range(4)]
                og = [sb(at, "og_%d" % i, [128, 4, 128], F32) for i in range(3)]
                sm2 = [sb(at, "sm2_%d" % i, [128, 8], F32) for i in range(4)]
                sqt = [sb(at, "sqt%d" % i, [128, 128], F32) for i in range(2)]
                cnt2 = {}

                def rr2(lst):
                    k = cnt2.get(id(lst), 0) + 1
                    cnt2[id(lst)] = k
                    return lst[k % len(lst)]
                set_rot([4, 5, 6, 7])
                accb = banks[0:4]
                acc_i = [0]
                deferred = []
                wab1 = [sb(at, "wab1_%d" % i, [128, 8, 256], BF16) for i in range(4)]
                mods1 = {"dma": 0, "mm": 0}

                def mods1_mm():
                    while mods1["mm"] < mods1["dma"]:
                        jb = mods1["mm"]
                        mods_block(1, jb, *wab1[jb % 4], nck=2)
                        mods1["mm"] += 1

                def mods1_dma():
                    for _ in range(4):
                        if mods1["dma"] < 24:
                            jb = mods1["dma"]
                            mods_dma(1, jb, *wab1[jb % 4], nck=2)
                            mods1["dma"] += 1

                def mods1_step():
                    mods1_mm()
                    mods1_dma()
                for i in range(len(sm2)):
                    T.op("dve", lambda i=i: nc.vector.memset(sm2[i][0][:, :], 1.0), writes=[sm2[i][1]])

                def attn_qgroup(K_, KR, prow, Q_, QR, q0, n, V_, VR, voffs, nch, nctx, scale, accs, qmap=None):
                    def qk(ck):
                        b, bR = nb()
                        v = 0 if ck < nctx else 1
                        rhs = Q_[prow, v, q0:q0 + n] if qmap is None else Q_[prow, v, qmap, q0:q0 + n]
                        T.op("pe", lambda: nc.tensor.matmul(b[:, 0:n], lhsT=K_[prow, ck * 128:(ck + 1) * 128], rhs=rhs, start=True, stop=True),
                             reads=[KR, QR], writes=[bR])
                        return b, bR
                    LA = 2
                    pend = [qk(i) for i in range(min(LA, nch))]
                    for ck in range(nch):
                        b, bR = pend.pop(0)
                        pt, ptR = rr2(PT)
                        T.op("act", lambda b=b, pt=pt: nc.scalar.activation(out=pt[:, 0:n], in_=b[:, 0:n], func=AF.Exp, scale=scale), reads=[bR], writes=[ptR], big=(n >= 256))
                        if ck + LA < nch:
                            pend.append(qk(ck + LA))
                        if ck == min(6, nch - 1):
                            while deferred:
                                deferred.pop(0)()
                        fns = []
                        for pi, off in enumerate(voffs):
                            fns.append(lambda pi=pi, off=off, pt=pt, ck=ck: nc.tensor.matmul(accs[pi][0][:, 0:n], lhsT=V_[:, ck, off:off + 128], rhs=pt[:, 0:n],
                                                                                            start=(ck == 0), stop=(ck == nch - 1)))
                        T.op("pe", fns, reads=[ptR, VR], writes=[a[1] for a in accs])

                def evac_tm(accs, n, nparts):
                    nsub = (n + 127) // 128
                    of_, ofR = rr2(ofm)
                    for pi in range(nparts):
                        a, aR = accs[pi]
                        T.op("dve", lambda a=a, pi=pi: nc.vector.tensor_copy(out=of_[:, pi, 0:n], in_=a[:, 0:n]), reads=[aR], writes=[ofR], big=(n >= 256))
                    ot_, otR = rr2(otm)
                    W = nparts * 128
                    per = 512 // W
                    for s0 in range(0, nsub, per):
                        ss = list(range(s0, min(nsub, s0 + per)))
                        tb, tbR = nb()
                        fns = []
                        for j, s in enumerate(ss):
                            qs = min(128, n - s * 128)
                            for pi in range(nparts):
                                fns.append(lambda j=j, s=s, qs=qs, pi=pi: nc.tensor.transpose(out=tb[0:qs, j * W + pi * 128:j * W + (pi + 1) * 128],
                                                                                             in_=of_[:, pi, s * 128:s * 128 + qs], identity=ident[:, :]))
                        T.op("pe", fns, reads=[ofR, identR], writes=[tbR])
                        qs0 = min(128, n - ss[0] * 128)
                        T.op("dve", lambda ss=ss, qs0=qs0: nc.vector.tensor_copy(out=ot_[0:qs0, ss[0]:ss[-1] + 1, 0:W],
                                                                             in_=tb[0:qs0, 0:len(ss) * W].rearrange("p (s w) -> p s w", w=W)), reads=[tbR], writes=[otR])
                    return ot_, otR

                def run_attention(name, QTd, QTm, KTd, KTm, Vd, Vm, cat, catR, qgroups, kcol0, nk, nctx, nvar, nqtot, qcol0):
                    nch = nk // 128
                    nctx_ = nctx if nvar == 2 else nch + 1
                    for hh in range(4):
                        K_, KR = rr2(Kb)
                        V_, VR = rr2(Vb)
                        Q_, QR = rr2(Qb)
                        T.dma("pool", K_[:, 0:nk], KTd[hh, :, kcol0:kcol0 + nk], writes=[KR])
                        T.dma("pool", V_[:, 0:nch, :], Vd[kcol0:kcol0 + nk, hh, :].rearrange("(c p) e -> p c e", p=128), writes=[VR])
                        for v in range(nvar):
                            T.dma("pool", Q_[0:64, v, 0, 0:nqtot], QTd[v, hh, 0:64, qcol0:qcol0 + nqtot], writes=[QR])
                            T.dma("pool", Q_[64:128, v, 1, 0:nqtot], QTd[v, hh, 64:128, qcol0:qcol0 + nqtot], writes=[QR])
                        for (q0, n) in qgroups:
                            nsub = (n + 127) // 128
                            ogt, ogR = rr2(og)
                            oa = []
                            for m in range(2):
                                accs = [accb[2 * m], accb[2 * m + 1]]
                                attn_qgroup(K_, KR, slice(0, 128), Q_, QR, q0, n, V_, VR, [0, 128], nch, nctx_, 0.125, accs, qmap=m)
                                oa.append(evac_tm(accs, n, 2))
                            (oa0, oa0R), (oa1, oa1R) = oa
                            sm, smR = rr2(sm2)
                            for s in range(nsub):
                                qs = min(128, n - s * 128)
                                v0 = oa0[0:qs, s, :].rearrange("p (t e) -> p t e", t=2)[:, :, 0:64]
                                v1 = oa1[0:qs, s, :].rearrange("p (t e) -> p t e", t=2)[:, :, 0:64]
                                ov = ogt[0:qs, s, :].rearrange("p (t e) -> p t e", t=2)
                                T.op("dve", lambda: nc.vector.reciprocal(out=sm[0:qs, 0:1], in_=oa0[0:qs, s, 64:65]), reads=[oa0R], writes=[smR])
                                T.op("dve", lambda: nc.vector.reciprocal(out=sm[0:qs, 1:2], in_=oa1[0:qs, s, 64:65]), reads=[oa1R], writes=[smR])
                                T.op("dve", lambda: nc.vector.tensor_tensor(out=sm[0:qs, 1:2], in0=sm[0:qs, 1:2], in1=lamt[0:qs, 4:5], op=ALU.mult), reads=[smR, lamtR], writes=[smR])
                                T.op("dve", lambda: nc.vector.tensor_scalar(out=v0, in0=v0, scalar1=sm[0:qs, 0:1], scalar2=None, op0=ALU.mult),
                                     reads=[smR, oa0R], writes=[oa0R])
                                T.op("dve", lambda: nc.vector.scalar_tensor_tensor(out=ov, in0=v1, scalar=sm[0:qs, 1:2], in1=v0,
                                                                                  op0=ALU.mult, op1=ALU.add), reads=[smR, oa0R, oa1R], writes=[ogR])
                                sq_, sqR_ = rr2(sqt)
                                T.op("dve", lambda: nc.vector.tensor_tensor(out=sq_[0:qs, :], in0=ogt[0:qs, s, :], in1=ogt[0:qs, s, :], op=ALU.mult), reads=[ogR], writes=[sqR_])
                                T.op("dve", lambda: nc.vector.reduce_sum(out=sm[0:qs, 4 + s:5 + s], in_=sq_[0:qs, :], axis=AX.X), reads=[sqR_], writes=[smR])
                            T.op("dve", lambda: nc.vector.tensor_scalar(out=sm[:, 4:4 + nsub], in0=sm[:, 4:4 + nsub], scalar1=1.0 / 128, scalar2=EPS, op0=ALU.mult, op1=ALU.add), reads=[smR], writes=[smR])

                            def fin(sm=sm, smR=smR, ogt=ogt, ogR=ogR, n=n, nsub=nsub, q0=q0, hh=hh):
                                T.op("act", lambda: nc.scalar.activation(out=sm[:, 4:4 + nsub], in_=sm[:, 4:4 + nsub], func=AF.Ln), reads=[smR], writes=[smR])
                                T.op("act", lambda: nc.scalar.activation(out=sm[:, 4:4 + nsub], in_=sm[:, 4:4 + nsub], func=AF.Exp, scale=-0.5), reads=[smR], writes=[smR])
                                for s in range(nsub):
                                    qs = min(128, n - s * 128)
                                    T.op("dve", lambda: nc.vector.tensor_scalar(out=ogt[0:qs, s, :], in0=ogt[0:qs, s, :], scalar1=sm[0:qs, 4 + s:5 + s], scalar2=None, op0=ALU.mult), reads=[smR, ogR], writes=[ogR])
                                nf = n // 128
                                r0 = qcol0 + q0
                                if nf > 0:
                                    T.dma("sp", cat[r0:r0 + nf * 128, hh * 128:(hh + 1) * 128].rearrange("(s p) e -> p s e", p=128), ogt[:, 0:nf, :], reads=[ogR], writes=[catR])
                                if n % 128:
                                    T.dma("sp", cat[r0 + nf * 128:r0 + n, hh * 128:(hh + 1) * 128], ogt[0:n - nf * 128, nf, :], reads=[ogR], writes=[catR])
                            deferred.append(fin)
                    while deferred:
                        deferred.pop(0)()
                    for hh in range(8):
                        if nvar == 2:
                            mods1_mm()
                        K_, KR = rr2(Kb)
                        V_, VR = rr2(Vb)
                        Q_, QR = rr2(Qm)
                        T.dma("pool", K_[0:96, 0:nk], KTm[hh, :, kcol0:kcol0 + nk], writes=[KR])
                        T.dma("pool", V_[:, 0:nch, 0:128], Vm[kcol0:kcol0 + nk, hh, :].rearrange("(c p) e -> p c e", p=128), writes=[VR])
                        for v in range(nvar):
                            T.dma("pool", Q_[0:96, v, 0:nqtot], QTm[v, hh, :, qcol0:qcol0 + nqtot], writes=[QR])
                        if nvar == 2:
                            mods1_dma()
                        for (q0, n) in qgroups:
                            nsub = (n + 127) // 128
                            ogt, ogR = rr2(og)
                            acc_i[0] = (acc_i[0] + 1) % 4
                            accs = [accb[acc_i[0]]]
                            attn_qgroup(K_, KR, slice(0, 96), Q_, QR, q0, n, V_, VR, [0], nch, nctx_, 96.0 ** -0.5, accs)
                            oat, oaR = evac_tm(accs, n, 1)
                            for s in range(nsub):
                                qs = min(128, n - s * 128)
                                sm, smR = rr2(sm2)
                                T.op("dve", lambda: nc.vector.reciprocal(out=sm[0:qs, 0:1], in_=oat[0:qs, s, 64:65]), reads=[oaR], writes=[smR])
                                T.op("dve", lambda: nc.vector.tensor_scalar(out=ogt[0:qs, s, 0:64], in0=oat[0:qs, s, 0:64], scalar1=sm[0:qs, 0:1], scalar2=None, op0=ALU.mult),
                                     reads=[oaR, smR], writes=[ogR])
                            c0 = 512 + hh * 64
                            nf = n // 128
                            r0 = qcol0 + q0
                            if nf > 0:
                                T.dma("sp", cat[r0:r0 + nf * 128, c0:c0 + 64].rearrange("(s p) e -> p s e", p=128), ogt[:, 0:nf, 0:64], reads=[ogR], writes=[catR])
                            if n % 128:
                                T.dma("sp", cat[r0 + nf * 128:r0 + n, c0:c0 + 64], ogt[0:n - nf * 128, nf, 0:64], reads=[ogR], writes=[catR])
                namesP = {k: k + "_P" for k in ("QTd", "QTm", "KTd", "KTm", "Vd", "Vm")}
                namesS = {k: k + "_S" for k in ("QTd", "QTm", "KTd", "KTm", "Vd", "Vm")}
                for sidx in range(2):
                    run_attention(namesP, QTd_P, QTm_P, KTd_P, KTm_P, Vd_P, Vm_P, cat_P, R["cat_P"], [(0, 256)], sidx * 256, 256, 0, 1, 256, sidx * 256)
                run_attention(namesS, QTd_S, QTm_S, KTd_S, KTm_S, Vd_S, Vm_S, cat_S, R["cat_S"], [(0, 352), (352, 352), (704, 352)], 0, NKS, 2, 2, NQ, 0)
                while mods1["mm"] < 24:
                    mods1_step()
                set_rot(list(range(8)))

            T.barrier()
            derive_cols(1)
            stage("attn")
            groupsP = [(0, 512)]
            groupsS0 = [(0, 512), (512, 512), (1024, 32)]
            with contextlib.ExitStack() as op_:
                xst[0] = sb(op_, "xst0b", [128, 4, D], F32)
                xst[1] = sb(op_, "xst1b", [128, 4, D], F32)
                wo, woR = load_w(op_, "wo", w_att_out_d, 8, D)
                catT, catTR = sb(op_, "catT", [128, 8, 512], BF16)
                lnt = mk_ln_tmps(op_) if False else None
                sq = [sb(op_, "lsq%d" % i, [128, 512], F32) for i in range(2)]
                tA = [sb(op_, "ltA%d" % i, [128, 512], F32) for i in range(2)]
                mean, meanR = sb(op_, "lmean", [128, 512], F32)
                rstd, rstdR = sb(op_, "lrstd", [128, 512], F32)
                lnt = ([s[0] for s in sq], [s[1] for s in sq], mean, meanR, rstd, rstdR, [s[0] for s in tA], [s[1] for s in tA])

                pend_ln = []

                def mixer_out(cat, catR, xr, xrR, hdst, hdstR, groups, l, cd, W, WR):
                    for (s0, n) in groups:
                        def ev_cat(c, b, bR):
                            sc = cols[:, 32, 0:1] if (l == 0 and c < 4) else cols[:, 33, 0:1]
                            T.op("act", lambda: nc.scalar.activation(out=catT[:, c, 0:n], in_=b[:, 0:n], func=AF.Copy, scale=sc), reads=[bR, colsR], writes=[catTR])
                        load_T(cat[s0:s0 + n, :], n, ev_cat)

                        def ev_y(j, b, bR):
                            T.op("dve", lambda: nc.vector.scalar_tensor_tensor(out=xr[:, j, s0:s0 + n], in0=b[:, 0:n], scalar=mods[:, l, 16 + j:17 + j, cd], in1=xr[:, j, s0:s0 + n],
                                                                              op0=ALU.mult, op1=ALU.add), reads=[bR, modsR, xrR], writes=[xrR], big=(n >= 256))
                        lin_fm(W, WR, 0, 8, catT, catTR, 0, n, ev_y)
                        while pend_ln:
                            ln1(*pend_ln.pop(0))
                        pend_ln.append((xr, xrR, hdst, hdstR, s0, n, l, cd))

                def ln1(xr, xrR, hdst, hdstR, s0, n, l, cd):
                    outs = [
                        ("act", lambda c, xn: act_affine(hdst[:, c, s0:s0 + n], xn, cols[:, CI(l, cd, 0), c:c + 1], cols[:, CI(l, cd, 1), c:c + 1]), [colsR], [hdstR]),
                        ("dve", lambda c, xn: dve_affine(xr[:, c, s0:s0 + n], xn, cols[:, CI(l, cd, 4), c:c + 1], cols[:, CI(l, cd, 5), c:c + 1]), [colsR], [xrR]),
                    ]
                    ln_fm(lnt, xr, xrR, 8, s0, n, onesD, onesDR, outs)

                mixer_out(cat_P, R["cat_P"], xrP, xrPR, hP, hPR, groupsP, 0, 0, wo, woR)
                mixer_out(cat_S, R["cat_S"], xrS, xrSR, hS, hSR, groupsS0, 0, 1, wo, woR)
                while pend_ln:
                    ln1(*pend_ln.pop(0))

        T.barrier()
        stage("mix0", [("xrP", xrP[:].rearrange("p a b -> p (a b)"), [128, 8 * NP], [xrPR]), ("xrS", xrS[:].rearrange("p a b -> p (a b)"), [128, 8 * NQ], [xrSR])])
        def mlp_layer(l, work, final):
            with contextlib.ExitStack() as ms:
                w1 = [sb(ms, "w1_%d" % i, [128, 8, 1024], BF16) for i in range(2)]
                w2 = [sb(ms, "w2_%d" % i, [128, 8, 1024], BF16) for i in range(2)]
                hid = [sb(ms, "hid%d" % i, [128, 8, 512], BF16) for i in range(2)]
                rl = [sb(ms, "rl%d" % i, [128, 512], F32) for i in range(2)]
                sq = [sb(ms, "msq%d" % i, [128, 512], F32) for i in range(2)]
                tA = [sb(ms, "mtA%d" % i, [128, 512], F32) for i in range(2)]
                mean, meanR = sb(ms, "mmean", [128, 512], F32)
                rstd, rstdR = sb(ms, "mrstd", [128, 512], F32)
                lnt2 = ([s[0] for s in sq], [s[1] for s in sq], mean, meanR, rstd, rstdR, [s[0] for s in tA], [s[1] for s in tA])
                k = 0
                for q in range(4):
                    wa, waR = w1[q % 2]
                    wb, wbR = w2[q % 2]
                    T.dma("pool", wa[:, :, :], w_mlp_in_d[l].rearrange("(c p) n -> p c n", p=128)[:, :, q * 1024:(q + 1) * 1024], writes=[waR])
                    T.dma("pool", wb[:, :, :], w_mlp_out_d[l, q * 1024:(q + 1) * 1024, :].rearrange("(c p) n -> p c n", p=128), writes=[wbR])
                    for (xr, xrR, h, hR, s0, n, cd) in work:
                        hd, hdR = hid[k % 2]
                        k += 1

                        def ev_h(j, b, bR):
                            r_, rR_ = rl[j % 2]
                            T.op("act", lambda: nc.scalar.activation(out=r_[:, 0:n], in_=b[:, 0:n], func=AF.Relu), reads=[bR], writes=[rR_], big=(n >= 256))
                            T.op("dve", lambda: nc.vector.tensor_tensor(out=hd[:, j, 0:n], in0=b[:, 0:n], in1=r_[:, 0:n], op=ALU.mult), reads=[bR, rR_], writes=[hdR], big=(n >= 256))
                        lin_fm(wa, waR, 0, 8, h, hR, s0, n, ev_h)

                        def ev_y(j, b, bR):
                            T.op("dve", lambda: nc.vector.scalar_tensor_tensor(out=xr[:, j, s0:s0 + n], in0=b[:, 0:n], scalar=mods[:, l, 40 + j:41 + j, cd], in1=xr[:, j, s0:s0 + n],
                                                                              op0=ALU.mult, op1=ALU.add), reads=[bR, modsR, xrR], writes=[xrR], big=(n >= 256))
                        lin_fm(wb, wbR, 0, 8, hd, hdR, 0, n, ev_y)
                for (xr, xrR, h, hR, s0, n, cd) in work:
                    outs = [("dve", lambda c, xn, xr=xr, s0=s0, n=n, cd=cd: dve_affine(xr[:, c, s0:s0 + n], xn, cols[:, CI(l, cd, 6), c:c + 1], cols[:, CI(l, cd, 7), c:c + 1]), [colsR], [xrR])]
                    if not final:
                        outs.append(("act", lambda c, xn, h=h, s0=s0, n=n, cd=cd: act_affine(h[:, c, s0:s0 + n], xn, cols[:, CI(l, cd, 2), c:c + 1], cols[:, CI(l, cd, 3), c:c + 1]), [colsR], [hR]))
                    ln_fm(lnt2, xr, xrR, 8, s0, n, onesD, onesDR, outs)

        mlp_layer(0, [(xrP, xrPR, hP, hPR, 0, 512, 0), (xrS, xrSR, hS, hSR, 0, 512, 1), (xrS, xrSR, hS, hSR, 512, 512, 1), (xrS, xrSR, hS, hSR, 1024, 32, 1)], False)

        T.barrier()
        stage("mlp0", [("xrP", xrP[:].rearrange("p a b -> p (a b)"), [128, 8 * NP], [xrPR]), ("xrS", xrS[:].rearrange("p a b -> p (a b)"), [128, 8 * NQ], [xrSR])])
        with contextlib.ExitStack() as l1:
            wC, wCR = load_w(l1, "wC", w_conv_in_d, 8, 1536)
            wco, wcoR = load_w(l1, "wco", w_conv_out_d, 8, D)
            wpl, wplR = sb(l1, "wpl", [128, 4, 128], BF16)
            T.dma("pool", wpl[:, :, :], w_pool_d.rearrange("g c e -> c g e"), writes=[wplR])
            LB = NQ + 32
            ub, ubR = sb(l1, "ub", [128, 4, LB], BF16)
            pzb = [sb(l1, "pzb%d" % i, [128, LB], F32) for i in range(1)]
            Ta, TaR = sb(l1, "Ta", [128, LB], F32)
            Tb, TbR = sb(l1, "Tb", [128, LB], F32)
            ict = [sb(l1, "ict%d" % i, [128, NOWN], F32) for i in range(1)]
            sg = [sb(l1, "sg%d" % i, [128, 512], F32) for i in range(1)]
            dgs = [sb(l1, "dg%d" % i, [128, CONVW, 128], BF16) for i in range(2)]
            dtl = [sb(l1, "dtl%d" % i, [128, 512], BF16) for i in range(1)]
            c1T, c1TR = sb(l1, "c1T", [128, 8, NOWN], BF16)
            sq = [sb(l1, "csq%d" % i, [128, 512], F32) for i in range(2)]
            tA = [sb(l1, "ctA%d" % i, [128, 512], F32) for i in range(2)]
            mean, meanR = sb(l1, "cmean", [128, 512], F32)
            rstd, rstdR = sb(l1, "crstd", [128, 512], F32)
            lnt3 = ([s[0] for s in sq], [s[1] for s in sq], mean, meanR, rstd, rstdR, [s[0] for s in tA], [s[1] for s in tA])

            def conv_pool(h, hR, xr, xrR, cd, in_groups, segs, ic_d, ntot, is_sample):
                boff = in_groups[0][2]
                T.op("dve", lambda: nc.vector.memset(ub[:, :, :], 0.0), writes=[ubR])
                for c in range(4):
                    for (s0, n, bo) in in_groups:
                        def ev_gate(j, b, bR):
                            s_, sR_ = sg[0]
                            T.op("act", lambda: nc.scalar.activation(out=s_[:, 0:n], in_=b[:, 0:n], func=AF.Sigmoid), reads=[bR], writes=[sR_])
                        lin_fm(wC, wCR, 512 + c * 128, 1, h, hR, s0, n, ev_gate)

                        def ev_a(j, b, bR):
                            s_, sR_ = sg[0]
                            T.op("dve", lambda: nc.vector.tensor_tensor(out=ub[:, c, bo:bo + n], in0=b[:, 0:n], in1=s_[:, 0:n], op=ALU.mult), reads=[bR, sR_], writes=[ubR])
                        lin_fm(wC, wCR, c * 128, 1, h, hR, s0, n, ev_a)
                if is_sample:
                    for c in range(4):
                        T.op("dve", lambda c=c: nc.vector.tensor_tensor(out=ub[:, c, 0:16], in0=ub[:, c, 0:16], in1=hmask[:, 0:16], op=ALU.mult), reads=[hmaskR, ubR], writes=[ubR])
                        T.op("dve", lambda c=c: nc.vector.tensor_tensor(out=ub[:, c, 16 + NOWN:32 + NOWN], in0=ub[:, c, 16 + NOWN:32 + NOWN], in1=hmask[:, 16:32], op=ALU.mult), reads=[hmaskR, ubR], writes=[ubR])
                for c in range(4):
                    dg, dgR = dgs[c % 2]
                    for j in range(CONVW):
                        T.op("dve", lambda j=j, c=c: nc.vector.tensor_scalar(out=dg[:, j, :], in0=identb[:, :], scalar1=vB[:, c, 4 + j:5 + j], scalar2=None, op0=ALU.mult),
                             reads=[identbR, vBR], writes=[dgR], big=True)
                    for (bo, oc0, ns, cbo) in segs:
                        for o0 in range(0, ns, 512):
                            n = min(512, ns - o0)
                            b, bR = nb()
                            fns = [lambda j=j, b=b, n=n, o0=o0: nc.tensor.matmul(b[:, 0:n], lhsT=dg[:, j, :], rhs=ub[:, c, cbo + o0 + j:cbo + o0 + j + n], start=(j == 0), stop=(j == CONVW - 1))
                                   for j in range(CONVW)]
                            T.op("pe", fns, reads=[dgR, ubR], writes=[bR])
                            T.op("act", lambda b=b, n=n, o0=o0, oc0=oc0: nc.scalar.activation(out=cvs[:, c, oc0 + o0:oc0 + o0 + n], in_=b[:, 0:n], func=AF.Identity, bias=vB[:, c, 0:1]),
                                 reads=[bR, vBR], writes=[cvsR])
                for o0 in range(0, ntot, 512):
                    n = min(512, ntot - o0)
                    outs = [("act", lambda c, xn, o0=o0, n=n: act_affine(c1T[:, c, o0:o0 + n], xn, vB[:, c, 1:2], vB[:, c, 2:3], func=AF.Silu), [vBR], [c1TR])]
                    ln_fm(lnt3, cvs, cvsR, 4, o0, n, onesH, onesHR, outs)
                for g in range(4):
                    pz, pzR = pzb[0]
                    ic, icR = ict[0]
                    T.dma("sp", ic[:, 0:ntot], ic_d[g:g + 1, :].to_broadcast([128, ntot]), writes=[icR])
                    T.op("dve", lambda pz=pz: nc.vector.memset(pz[:, :], 0.0), writes=[pzR])
                    for (s0, n, bo) in in_groups:
                        def ev_pz(j, b, bR):
                            T.op("act", lambda: nc.scalar.copy(out=pz[:, bo:bo + n], in_=b[:, 0:n]), reads=[bR], writes=[pzR])
                        lin_fm(wC, wCR, 1024 + g * 128, 1, h, hR, s0, n, ev_pz)
                    if is_sample:
                        T.op("dve", lambda pz=pz: nc.vector.tensor_tensor(out=pz[:, 0:16], in0=pz[:, 0:16], in1=hmask[:, 0:16], op=ALU.mult), reads=[hmaskR, pzR], writes=[pzR])
                        T.op("dve", lambda pz=pz: nc.vector.tensor_tensor(out=pz[:, 16 + NOWN:32 + NOWN], in0=pz[:, 16 + NOWN:32 + NOWN], in1=hmask[:, 16:32], op=ALU.mult), reads=[hmaskR, pzR], writes=[pzR])
                    L = LB
                    srcb, srcR = pz, pzR
                    sh = 1
                    dst_list = [(Ta, TaR), (Tb, TbR)]
                    Lc = L
                    for k in range(g + 1):
                        dstb, dstR = dst_list[k % 2]
                        Lc = Lc - sh
                        T.op("dve", lambda srcb=srcb, dstb=dstb, Lc=Lc, sh=sh: nc.vector.tensor_tensor(out=dstb[:, 0:Lc], in0=srcb[:, 0:Lc], in1=srcb[:, sh:sh + Lc], op=ALU.add),
                             reads=[srcR], writes=[dstR])
                        srcb, srcR = dstb, dstR
                        sh *= 2
                    w = 2 ** (g + 1)
                    for (bo, oc0, ns, cbo) in segs:
                        for o0 in range(0, ns, 512):
                            n = min(512, ns - o0)
                            d_, dR_ = dtl[0]
                            a1, a1R = tA[0][0], tA[0][1]
                            T.op("dve", lambda: nc.vector.tensor_tensor(out=a1[:, 0:n], in0=srcb[:, bo + o0 - w // 2:bo + o0 - w // 2 + n], in1=ic[:, oc0 + o0:oc0 + o0 + n], op=ALU.mult),
                                 reads=[srcR, icR], writes=[a1R])
                            T.op("dve", lambda: nc.vector.tensor_tensor(out=d_[:, 0:n], in0=a1[:, 0:n], in1=pz[:, bo + o0:bo + o0 + n], op=ALU.subtract), reads=[a1R, pzR], writes=[dR_])
                            b, bR = nb()
                            T.op("pe", lambda b=b: nc.tensor.matmul(b[:, 0:n], lhsT=wpl[:, g, :], rhs=d_[:, 0:n], start=True, stop=True), reads=[dR_, wplR], writes=[bR])
                            T.op("act", lambda b=b: nc.scalar.activation(out=c1T[:, 4 + g, oc0 + o0:oc0 + o0 + n], in_=b[:, 0:n], func=AF.Copy, scale=vB[:, g, 3:4]), reads=[bR, vBR], writes=[c1TR])

            cvs, cvsR = sb(l1, "cvs", [128, 4, NOWN], F32)

            def mixer1_out(xr, xrR, hdst, hdstR, out_groups, cd):
                for (cc0, x0, n) in out_groups:
                    def ev_y(j, b, bR):
                        T.op("dve", lambda: nc.vector.scalar_tensor_tensor(out=xr[:, j, x0:x0 + n], in0=b[:, 0:n], scalar=mods[:, 1, 16 + j:17 + j, cd], in1=xr[:, j, x0:x0 + n],
                                                                          op0=ALU.mult, op1=ALU.add), reads=[bR, modsR, xrR], writes=[xrR], big=(n >= 256))
                    lin_fm(wco, wcoR, 0, 8, c1T, c1TR, cc0, n, ev_y)
                    outs = [
                        ("act", lambda c, xn: act_affine(hdst[:, c, x0:x0 + n], xn, cols[:, CI(1, cd, 0), c:c + 1], cols[:, CI(1, cd, 1), c:c + 1]), [colsR], [hdstR]),
                        ("dve", lambda c, xn: dve_affine(xr[:, c, x0:x0 + n], xn, cols[:, CI(1, cd, 4), c:c + 1], cols[:, CI(1, cd, 5), c:c + 1]), [colsR], [xrR]),
                    ]
                    ln_fm(lnt3, xr, xrR, 8, x0, n, onesD, onesDR, outs)

            conv_pool(hP, hPR, xrP, xrPR, 0, [(0, 256, 16), (256, 256, 16 + 256 + 16)],
                      [(16, 0, 256, 1), (16 + 256 + 16, 256, 256, 1 + 256 + 16)], icP_d, NP, False)
            mixer1_out(xrP, xrPR, hP, hPR, [(0, 0, 512)], 0)
            conv_pool(hS, hSR, xrS, xrSR, 1, [(0, 512, 0), (512, 512, 512), (1024, 32, 1024)],
                      [(16, 0, NOWN, 1)], icS_d, NOWN, True)
            mixer1_out(xrS, xrSR, hS, hSR, [(0, 16, 512), (512, 528, 512)], 1)

        T.barrier()
        stage("mix1", [("xrP", xrP[:].rearrange("p a b -> p (a b)"), [128, 8 * NP], [xrPR]), ("xrS", xrS[:].rearrange("p a b -> p (a b)"), [128, 8 * NQ], [xrSR])])
        mlp_layer(1, [(xrP, xrPR, hP, hPR, 0, 512, 0), (xrS, xrSR, hS, hSR, 16, 512, 1), (xrS, xrSR, hS, hSR, 528, 512, 1)], True)

        T.barrier()
        with contextlib.ExitStack() as fo:
            ot = [sb(fo, "ot%d" % i, [128, D], F32) for i in range(3)]
            k = 0
            for (xr, xrR, x0, ntok, dst) in ((xrP, xrPR, 0, NP, yp_d), (xrS, xrSR, 16, NOWN, ys_d)):
                for t in range(ntok // 128):
                    o, oR = ot[k % 3]
                    k += 1
                    for half in range(2):
                        b, bR = nb()
                        fns = [lambda c=c, b=b: nc.tensor.transpose(out=b[:, (c % 4) * 128:(c % 4 + 1) * 128], in_=xr[:, c, x0 + t * 128:x0 + (t + 1) * 128], identity=ident[:, :])
                               for c in range(half * 4, half * 4 + 4)]
                        T.op("pe", fns, reads=[xrR, identR], writes=[bR])
                        if half == 0:
                            T.op("act", lambda b=b, o=o: nc.scalar.copy(out=o[:, 0:512], in_=b[:, 0:512]), reads=[bR], writes=[oR])
                        else:
                            T.op("dve", lambda b=b, o=o: nc.vector.tensor_copy(out=o[:, 512:1024], in_=b[:, 0:512]), reads=[bR], writes=[oR])
                    T.dma("sp", dst[t * 128:(t + 1) * 128, :], o[:, :], reads=[oR], writes=[], is_output=True)
        T.finish()
    return nc


def mk_ln_tmps(stack):
    return None


def _rope_tables(pos, nrows, half):
    T_ = pos.shape[0]
    row = (pos // 64).astype(np.float64)
    col = (pos % 64).astype(np.float64)
    inv = 10000.0 ** (-np.arange(half, dtype=np.float64) / float(half))
    out = np.zeros((2, nrows, T_), np.float32)
    blk = 4 * half
    for i in range(nrows):
        d = i % blk
        p = row if d < 2 * half else col
        kk = d % half
        ang = p * inv[kk]
        sgn = -1.0 if (d % (2 * half)) < half else 1.0
        out[0, i] = np.cos(ang)
        out[1, i] = sgn * np.sin(ang)
    return out


def _perm(n, half):
    m = np.zeros((n, n), np.float32)
    for i in range(n):
        j = i + half if (i % (2 * half)) < half else i - half
        m[j, i] = 1.0
    return m


def _invcnt(T_, pos):
    out = np.zeros((4, pos.shape[0]), np.float32)
    for g, w in enumerate((2, 4, 8, 16)):
        lo = np.maximum(pos - w // 2, 0)
        hi = np.minimum(pos + w // 2 - 1, T_ - 1)
        out[g] = (1.0 / (hi - lo + 1).astype(np.float64)).astype(np.float32)
    return out


_NC_CACHE = {}


def kernel(x_prompt, x_sample, cache_diff_k, cache_diff_v, cache_mla_ckv, cache_mla_krope, c, c_ctx,
           w_ada, b_ada, ln_mix_g, ln_mix_b, ln_mlp_g, ln_mlp_b, w_mlp_in, w_mlp_out,
           w_att_in, w_uq, w_ukv, q_norm_g, kv_norm_g, lam_q1, lam_k1, lam_q2, lam_k2, diff_norm_g, w_att_out,
           w_conv_in, conv_w, conv_b, conv_norm_g, conv_norm_b, w_pool, pool_scale, w_conv_out):
    f = lambda a: np.ascontiguousarray(np.asarray(a, dtype=np.float32))
    x_prompt, x_sample = f(x_prompt), f(x_sample)
    if "nc" not in _NC_CACHE:
        _NC_CACHE["nc"] = build()
    nc = _NC_CACHE["nc"]
    pos_all = np.arange(TS)
    ropek = _rope_tables(pos_all, 128, 16)
    ropekpe = _rope_tables(pos_all, 32, 8)
    permd = _perm(128, 16)
    permpe = _perm(32, 8)
    ident = np.eye(128, dtype=np.float32)
    vecB = np.concatenate([f(conv_b)[0:1], f(conv_norm_g)[0:1], f(conv_norm_b)[0:1], f(pool_scale)[0:1], f(conv_w)[0]], axis=0)
    vecC = np.zeros((3, 256), np.float32)
    vecC[0] = f(q_norm_g)[0]
    vecC[1, :128] = f(kv_norm_g)[0]
    vecC[2, :128] = f(diff_norm_g)[0]
    lamv = np.concatenate([f(lam_q1)[0], f(lam_k1)[0], f(lam_q2)[0], f(lam_k2)[0]])[None, :]
    icP = np.concatenate([_invcnt(SEQ, np.arange(SEQ))] * 2, axis=1)
    shared = {
        "vecB": vecB, "vecC": vecC, "lamv": lamv, "ident": ident, "permd": permd, "permpe": permpe,
        "ropek": ropek, "ropekpe": ropekpe, "icP": icP,
        "w_ada": f(w_ada), "w_mlp_in": f(w_mlp_in), "w_mlp_out": f(w_mlp_out), "w_att_in": f(w_att_in)[0],
        "w_uq": f(w_uq)[0], "w_ukv": f(w_ukv)[0], "w_att_out": f(w_att_out)[0], "w_conv_in": f(w_conv_in)[0],
        "w_pool": f(w_pool)[0], "w_conv_out": f(w_conv_out)[0],
    }
    in_maps = []
    for core in range(NCORES):
        bs, qt = core // 4, core % 4
        own0 = qt * NOWN
        xq = np.zeros((NQ, D), np.float32)
        lo, hi = own0 - HALO, own0 + NOWN + HALO
        slo, shi = max(lo, 0), min(hi, TS)
        xq[slo - lo:shi - lo] = x_sample[bs, slo:shi]
        posq = np.clip(np.arange(lo, hi), 0, TS - 1)
        hm = np.zeros((1, 32), np.float32)
        hm[0, :16] = 1.0 if qt > 0 else 0.0
        hm[0, 16:] = 1.0 if qt < 3 else 0.0
        vecA = np.concatenate([f(b_ada)[0].reshape(6, D), f(b_ada)[1].reshape(6, D),
                               f(ln_mix_g)[0:1], f(ln_mix_b)[0:1], f(ln_mlp_g)[0:1], f(ln_mlp_b)[0:1],
                               f(ln_mix_g)[1:2], f(ln_mix_b)[1:2], f(ln_mlp_g)[1:2], f(ln_mlp_b)[1:2],
                               f(c)[bs:bs + 1], f(c_ctx)[None, :]], axis=0)
        m = dict(shared)
        m.update({
            "xp": x_prompt[2 * core:2 * core + 2].reshape(NP, D),
            "xq": xq,
            "xkv": x_sample[bs],
            "ck": f(cache_diff_k)[bs, 0].reshape(PAST, 512),
            "cv": f(cache_diff_v)[bs, 0].reshape(PAST, 512),
            "cckv": f(cache_mla_ckv)[bs, 0],
            "ckr": f(cache_mla_krope)[bs, 0],
            "vecA": vecA,
            "ropeq": np.ascontiguousarray(ropek[:, :, posq]),
            "ropeqpe": np.ascontiguousarray(ropekpe[:, :, posq]),
            "hmask": hm,
            "icS": _invcnt(TS, np.arange(own0, own0 + NOWN)),
        })
        in_maps.append({k: np.ascontiguousarray(v) for k, v in m.items()})
    if _NC_CACHE.get("prep_only"):
        return in_maps
    res = run_bass_kernel_spmd(nc, in_maps, core_ids=list(range(NCORES)))
    rs = res.results
    y_prompt = np.stack([rs[i]["yp"] for i in range(NCORES)]).reshape(16, SEQ, D)
    y_sample = np.stack([rs[i]["ys"] for i in range(NCORES)]).reshape(2, TS, D)
    ndk = np.stack([rs[i]["ndk"] for i in range(NCORES)]).reshape(16, 1, SEQ, 4, 128)
    ndv = np.stack([rs[i]["ndv"] for i in range(NCORES)]).reshape(16, 1, SEQ, 4, 128)
    nckv = np.stack([rs[i]["nckv"] for i in range(NCORES)]).reshape(16, 1, SEQ, 128)
    nkr = np.stack([rs[i]["nkr"] for i in range(NCORES)]).reshape(16, 1, SEQ, 32)
    return (y_prompt.astype(np.float32), y_sample.astype(np.float32), ndk.astype(np.float32), ndv.astype(np.float32),
            nckv.astype(np.float32), nkr.astype(np.float32))
```

```python
import contextlib
import numpy as np
import concourse.bass as bass
import concourse.mybir as mybir
from concourse.bass_utils import run_bass_kernel_spmd

F32 = mybir.dt.float32
BF16 = mybir.dt.bfloat16
AF = mybir.ActivationFunctionType
ALU = mybir.AluOpType
AX = mybir.AxisListType

NCORES = 8
D = 1024
KC = 8
SEQ = 256
NP = 512
NOWN = 1024
HALO = 16
NQ = NOWN + 2 * HALO
TS = 4096
PAST = 256
NKS = PAST + TS
DFF = 4096
ALPHA = 4.0 ** 0.25
EPS = 1e-5
LAM_INIT0 = 0.8 - 0.6 * 1.0
CONVW = 31
SAME_ENGINE_SYNC = True


class Res:
    __slots__ = ("w", "r", "name", "excl")

    def __init__(self, name="", excl=False):
        self.w = None
        self.r = {}
        self.name = name
        self.excl = excl


class Tracker:
    NS = 8

    def __init__(self, nc, es):
        self.nc = nc
        self.eng = {"pe": nc.tensor, "act": nc.scalar, "dve": nc.vector, "pool": nc.gpsimd, "sp": nc.sync}
        self.sem = {e: es.enter_context(nc.semaphore("s_" + e)) for e in self.eng}
        self.cnt = {e: 0 for e in self.eng}
        self.seen = {e: {} for e in self.eng}
        self.dsem = {q: [es.enter_context(nc.semaphore("d%s%d" % (q, i))) for i in range(self.NS)] for q in ("sp", "pool", "act")}
        self.dcnt = {q: [0] * self.NS for q in self.dsem}
        self.dnext = {q: 0 for q in self.dsem}
        self.out_events = []

    def _wait(self, e, ev):
        key, sh, val = ev[0], ev[1], ev[2]
        if self.seen[e].get(key, 0) >= val:
            return
        self.eng[e].wait_ge(sh, val)
        self.seen[e][key] = val

    @staticmethod
    def _deps(reads, writes):
        evs = []
        for r in reads:
            if r.w is not None:
                evs.append(r.w)
            if r.excl:
                evs.extend(r.r.values())
        for w in writes:
            if w.w is not None:
                evs.append(w.w)
            evs.extend(w.r.values())
        return evs

    def op(self, e, fns, reads=(), writes=(), big=False):
        reads = [r for r in reads if r is not None]
        writes = [w for w in writes if w is not None]
        for ev in self._deps(reads, writes):
            if ev[0] == e and (e == "pe" or not SAME_ENGINE_SYNC or ev[3]):
                continue
            self._wait(e, ev)
        if not isinstance(fns, (list, tuple)):
            fns = [fns]
        ins = None
        for f in fns:
            ins = f()
        self.cnt[e] += 1
        ins.then_inc(self.sem[e], 1)
        ev = (e, self.sem[e], self.cnt[e], big)
        for r in reads:
            r.r[e] = ev
        for w in writes:
            w.w = ev
            w.r = {}

    def dma(self, q, out, in_, reads=(), writes=(), is_output=False):
        reads = [r for r in reads if r is not None]
        writes = [w for w in writes if w is not None]
        for ev in self._deps(reads, writes):
            self._wait(q, ev)
        i = self.dnext[q]
        self.dnext[q] = (i + 1) % self.NS
        key = "d%s%d" % (q, i)
        if self.dcnt[q][i] > 0:
            self._wait(q, (key, self.dsem[q][i], self.dcnt[q][i]))
        self.dcnt[q][i] += 16
        self.eng[q].dma_start(out=out, in_=in_).then_inc(self.dsem[q][i], 16)
        ev = (key, self.dsem[q][i], self.dcnt[q][i], False)
        for r in reads:
            r.r[key] = ev
        for w in writes:
            w.w = ev
            w.r = {}
        if is_output:
            self.out_events.append(ev)

    def barrier(self):
        evs = []
        for q in self.dsem:
            for i in range(self.NS):
                if self.dcnt[q][i] > 0:
                    evs.append(("d%s%d" % (q, i), self.dsem[q][i], self.dcnt[q][i]))
        for e in self.eng:
            if self.cnt[e] > 0:
                evs.append((e, self.sem[e], self.cnt[e]))
        for e in self.eng:
            for ev in evs:
                if ev[0] != e:
                    self._wait(e, ev)

    def finish(self):
        for q in self.dsem:
            for i in range(self.NS):
                if self.dcnt[q][i] > 0:
                    self._wait("sp", ("d%s%d" % (q, i), self.dsem[q][i], self.dcnt[q][i]))
        for e in self.eng:
            if e != "sp" and self.cnt[e] > 0:
                self._wait("sp", (e, self.sem[e], self.cnt[e]))


class _Stop(Exception):
    pass


def build(debug=None):
    try:
        return _build(debug)
    except _Stop as e:
        return e.args[0]


def _build(debug=None):
    nc = bass.Bass("TRN2", target_bir_lowering=False)

    def din(name, shape, dt=F32):
        return nc.dram_tensor(name, list(shape), dt, kind="ExternalInput").ap()

    def dout(name, shape, dt=F32):
        return nc.dram_tensor(name, list(shape), dt, kind="ExternalOutput").ap()

    def dscr(name, shape, dt):
        return nc.dram_tensor(name, list(shape), dt, kind="Internal").ap()

    xp_d = din("xp", [NP, D])
    xq_d = din("xq", [NQ, D])
    xkv_d = din("xkv", [TS, D])
    ck_d = din("ck", [PAST, 512])
    cv_d = din("cv", [PAST, 512])
    cckv_d = din("cckv", [PAST, 128])
    ckr_d = din("ckr", [PAST, 32])
    vecA_d = din("vecA", [22, D])
    vecB_d = din("vecB", [35, 512])
    vecC_d = din("vecC", [3, 256])
    lam_d = din("lamv", [1, 256])
    ident_d = din("ident", [128, 128])
    permd_d = din("permd", [128, 128])
    permpe_d = din("permpe", [32, 32])
    ropek_d = din("ropek", [2, 128, TS])
    ropekpe_d = din("ropekpe", [2, 32, TS])
    ropeq_d = din("ropeq", [2, 128, NQ])
    ropeqpe_d = din("ropeqpe", [2, 32, NQ])
    hmask_d = din("hmask", [1, 32])
    icS_d = din("icS", [4, NOWN])
    icP_d = din("icP", [4, NP])
    w_ada_d = din("w_ada", [2, D, 6 * D])
    w_mlp_in_d = din("w_mlp_in", [2, D, DFF])
    w_mlp_out_d = din("w_mlp_out", [2, DFF, D])
    w_att_in_d = din("w_att_in", [D, 1952])
    w_uq_d = din("w_uq", [256, 768])
    w_ukv_d = din("w_ukv", [128, 1024])
    w_att_out_d = din("w_att_out", [D, D])
    w_conv_in_d = din("w_conv_in", [D, 1536])
    w_pool_d = din("w_pool", [4, 128, 128])
    w_conv_out_d = din("w_conv_out", [D, D])

    yp_d = dout("yp", [NP, D])
    ys_d = dout("ys", [NOWN, D])
    ndk_d = dout("ndk", [NP, 512])
    ndv_d = dout("ndv", [NP, 512])
    nckv_d = dout("nckv", [NP, 128])
    nkr_d = dout("nkr", [NP, 32])

    QTd_P = dscr("QTd_P", [1, 4, 128, NP], BF16)
    KTd_P = dscr("KTd_P", [4, 128, NP], BF16)
    Vd_P = dscr("Vd_P", [NP, 4, 256], BF16)
    QTm_P = dscr("QTm_P", [1, 8, 96, NP], BF16)
    KTm_P = dscr("KTm_P", [8, 96, NP], BF16)
    Vm_P = dscr("Vm_P", [NP, 8, 128], BF16)
    QTd_S = dscr("QTd_S", [2, 4, 128, NQ], BF16)
    KTd_S = dscr("KTd_S", [4, 128, NKS], BF16)
    Vd_S = dscr("Vd_S", [NKS, 4, 256], BF16)
    QTm_S = dscr("QTm_S", [2, 8, 96, NQ], BF16)
    KTm_S = dscr("KTm_S", [8, 96, NKS], BF16)
    Vm_S = dscr("Vm_S", [NKS, 8, 128], BF16)
    cat_P = dscr("cat_P", [NP, D], F32)
    cat_S = dscr("cat_S", [NQ, D], F32)
    dbg = {}

    es = contextlib.ExitStack()
    with es:
        T = Tracker(nc, es)

        uniq = [0]

        def sb(stack, name, shape, dt):
            uniq[0] += 1
            t = stack.enter_context(nc.sbuf_tensor("sb%d_%s" % (uniq[0], name), list(shape), dt))
            return t, Res(name)

        banks = []
        for i in range(8):
            b = es.enter_context(nc.psum_tensor("bank%d" % i, [128, 512], F32))
            banks.append((b, Res("bank%d" % i, excl=True)))
        rot_state = {"i": 0, "set": list(range(8))}

        def nb():
            s = rot_state["set"]
            rot_state["i"] = (rot_state["i"] + 1) % len(s)
            return banks[s[rot_state["i"]]]

        def set_rot(lst):
            rot_state["set"] = list(lst)
            rot_state["i"] = 0

        R = {n: None for n in ["QTd_P", "KTd_P", "Vd_P", "QTm_P", "KTm_P", "Vm_P", "QTd_S", "KTd_S", "Vd_S",
                                 "QTm_S", "KTm_S", "Vm_S", "cat_P", "cat_S"]}
        NOR = Res("const")

        def dump(name, ap, shape, reads):
            d = dout("dbg_" + name, shape, F32)
            T.dma("pool", d, ap, reads=reads, writes=[], is_output=True)

        def stage(name, dumps=()):
            if debug == name:
                for (n_, ap_, shp_, rd_) in dumps:
                    dump(n_, ap_, shp_, rd_)
                T.finish()
                raise _Stop(nc)

        ident, identR = sb(es, "ident", [128, 128], F32)
        identb, identbR = sb(es, "identb", [128, 128], BF16)
        permd, permdR = sb(es, "permd", [128, 128], BF16)
        permpe, permpeR = sb(es, "permpe", [32, 32], BF16)
        onesD, onesDR = sb(es, "onesD", [128, 128], F32)
        onesH, onesHR = sb(es, "onesH", [128, 128], F32)
        vT, vTR = sb(es, "vT", [128, 8, 22], F32)
        vB, vBR = sb(es, "vB", [128, 4, 35], F32)
        vC, vCR = sb(es, "vC", [128, 2, 3], F32)
        mods, modsR = sb(es, "mods", [128, 2, 48, 2], F32)
        cols, colsR = sb(es, "cols", [128, 40, 8], F32)
        lamt, lamtR = sb(es, "lamt", [128, 8], F32)
        silc, silcR = sb(es, "silc", [128, 8, 2], BF16)
        hmask, hmaskR = sb(es, "hmask", [128, 32], F32)
        xrP, xrPR = sb(es, "xrP", [128, 8, NP], F32)
        xrS, xrSR = sb(es, "xrS", [128, 8, NQ], F32)
        hP, hPR = sb(es, "hP", [128, 8, NP], BF16)
        hS, hSR = sb(es, "hS", [128, 8, NQ], BF16)

        T.dma("sp", ident[:], ident_d[:, :], writes=[identR])
        T.dma("pool", identb[:], ident_d[:, :], writes=[identbR])
        T.dma("pool", permd[:], permd_d[:, :], writes=[permdR])
        T.dma("pool", permpe[:], permpe_d[:, :], writes=[permpeR])
        T.dma("sp", hmask[:], hmask_d.to_broadcast([128, 32]), writes=[hmaskR])
        T.op("dve", lambda: nc.vector.memset(onesD[:], 1.0 / D), writes=[onesDR])
        T.op("dve", lambda: nc.vector.memset(onesH[:], 1.0 / 512), writes=[onesHR])

        def load_w(stack, name, src, K, N, q="pool"):
            t, r = sb(stack, name, [128, K, N], BF16)
            sv = src.rearrange("(c p) n -> p c n", p=128)
            step = 1024
            for n0 in range(0, N, step):
                n1 = min(N, n0 + step)
                T.dma(q, t[:, :, n0:n1], sv[:, :, n0:n1], writes=[r])
            return t, r

        xst = [None, None]
        xsti = [0]

        def load_T(src, n, evac):
            nt = (n + 127) // 128
            xt, xtR = xst[xsti[0]]
            xsti[0] ^= 1
            nf = n // 128
            if nf > 0:
                T.dma("pool", xt[:, 0:nf, :], src[0:nf * 128, :].rearrange("(t p) f -> p t f", p=128), writes=[xtR])
            if n % 128:
                T.dma("pool", xt[0:n - nf * 128, nf, :], src[nf * 128:n, :], writes=[xtR])
            for c in range(KC):
                b, bR = nb()
                fns = []
                for t in range(nt):
                    rows = min(128, n - t * 128)
                    fns.append(lambda t=t, rows=rows, b=b, c=c: nc.tensor.transpose(
                        out=b[:, t * 128:t * 128 + rows], in_=xt[0:rows, t, c * 128:(c + 1) * 128],
                        identity=ident[0:rows, 0:rows]))
                T.op("pe", fns, reads=[xtR, identR], writes=[bR])
                evac(c, b, bR)

        def lin_fm(W, WR, c0, nchunk, src, srcR, s0, n, evac, K=KC, m=128):
            pend = None
            for j in range(nchunk):
                b, bR = nb()
                fns = []
                for kc in range(K):
                    fns.append(lambda kc=kc, b=b, j=j: nc.tensor.matmul(
                        b[0:m, 0:n], lhsT=W[:, kc, c0 + j * m:c0 + (j + 1) * m], rhs=src[:, kc, s0:s0 + n],
                        start=(kc == 0), stop=(kc == K - 1)))
                T.op("pe", fns, reads=[WR, srcR], writes=[bR])
                if pend is not None:
                    evac(*pend)
                pend = (j, b, bR)
            if pend is not None:
                evac(*pend)

        def lin_tm(W, WR, c0, ncol, src, srcR, s0, n, evac, K=KC):
            nt = (n + 127) // 128
            pend = None
            for t in range(nt):
                rows = min(128, n - t * 128)
                b, bR = nb()
                fns = []
                for kc in range(K):
                    fns.append(lambda kc=kc, b=b, t=t, rows=rows: nc.tensor.matmul(
                        b[0:rows, 0:ncol], lhsT=src[:, kc, s0 + t * 128:s0 + t * 128 + rows], rhs=W[:, kc, c0:c0 + ncol],
                        start=(kc == 0), stop=(kc == K - 1)))
                T.op("pe", fns, reads=[WR, srcR], writes=[bR])
                if pend is not None:
                    evac(*pend)
                pend = (t, rows, b, bR)
            if pend is not None:
                evac(*pend)

        def mods_dma(l, jb, wt, wtR, nck=4):
            wv = w_ada_d[l].rearrange("(c p) n -> p c n", p=128)
            T.dma("pool", wt[:, :, 0:nck * 128], wv[:, :, jb * nck * 128:(jb + 1) * nck * 128], writes=[wtR])

        def mods_block(l, jb, wt, wtR, nck=4):
            b, bR = nb()
            fns = []
            for jj in range(nck):
                for kc in range(8):
                    fns.append(lambda jj=jj, kc=kc: nc.tensor.matmul(b[:, 2 * jj:2 * jj + 2], lhsT=wt[:, kc, jj * 128:(jj + 1) * 128], rhs=silc[:, kc, :],
                                                                     start=(kc == 0), stop=(kc == 7)))
            T.op("pe", fns, reads=[wtR, silcR], writes=[bR])
            j0 = jb * nck
            sp_ = j0 // 8
            c0 = j0 % 8
            add1 = 1.0 if sp_ in (1, 4) else 0.0
            for cd in range(2):
                T.op("dve", lambda cd=cd: nc.vector.scalar_tensor_tensor(out=mods[:, l, j0:j0 + nck, cd], in0=b[:, cd:2 * nck:2], scalar=add1, in1=vT[:, c0:c0 + nck, l * 6 + sp_],
                                                                        op0=ALU.add, op1=ALU.add), reads=[bR, vTR], writes=[modsR])

        with contextlib.ExitStack() as ps:
            vA, vAR = sb(ps, "vecA", [22, D], F32)
            vBt, vBtR = sb(ps, "vecBt", [35, 512], F32)
            vCt, vCtR = sb(ps, "vecCt", [3, 256], F32)
            lamv, lamvR = sb(ps, "lamv", [128, 256], F32)
            T.dma("sp", vA[:], vecA_d[:, :], writes=[vAR])
            T.dma("sp", vBt[:], vecB_d[:, :], writes=[vBtR])
            T.dma("sp", vCt[:], vecC_d[:, :], writes=[vCtR])
            T.dma("sp", lamv[:], lam_d.to_broadcast([128, 256]), writes=[lamvR])
            for c in range(8):
                b, bR = nb()
                T.op("pe", lambda b=b, c=c: nc.tensor.transpose(out=b[:, 0:22], in_=vA[0:22, c * 128:(c + 1) * 128], identity=ident[0:22, 0:22]),
                     reads=[vAR, identR], writes=[bR])
                T.op("dve", lambda b=b, c=c: nc.vector.tensor_copy(out=vT[:, c, :], in_=b[:, 0:22]), reads=[bR], writes=[vTR])
            for c in range(4):
                b, bR = nb()
                T.op("pe", lambda b=b, c=c: nc.tensor.transpose(out=b[:, 0:35], in_=vBt[0:35, c * 128:(c + 1) * 128], identity=ident[0:35, 0:35]),
                     reads=[vBtR, identR], writes=[bR])
                T.op("dve", lambda b=b, c=c: nc.vector.tensor_copy(out=vB[:, c, :], in_=b[:, 0:35]), reads=[bR], writes=[vBR])
            for c in range(2):
                b, bR = nb()
                T.op("pe", lambda b=b, c=c: nc.tensor.transpose(out=b[:, 0:3], in_=vCt[0:3, c * 128:(c + 1) * 128], identity=ident[0:3, 0:3]),
                     reads=[vCtR, identR], writes=[bR])
                T.op("dve", lambda b=b, c=c: nc.vector.tensor_copy(out=vC[:, c, :], in_=b[:, 0:3]), reads=[bR], writes=[vCR])
            T.op("dve", lambda: nc.vector.tensor_tensor(out=lamv[:, 0:64], in0=lamv[:, 0:64], in1=lamv[:, 64:128], op=ALU.mult), reads=[lamvR], writes=[lamvR])
            T.op("dve", lambda: nc.vector.tensor_tensor(out=lamv[:, 128:192], in0=lamv[:, 128:192], in1=lamv[:, 192:256], op=ALU.mult), reads=[lamvR], writes=[lamvR])
            T.op("dve", lambda: nc.vector.reduce_sum(out=lamt[:, 0:1], in_=lamv[:, 0:64], axis=AX.X), reads=[lamvR], writes=[lamtR])
            T.op("dve", lambda: nc.vector.reduce_sum(out=lamt[:, 1:2], in_=lamv[:, 128:192], axis=AX.X), reads=[lamvR], writes=[lamtR])
            T.op("act", lambda: nc.scalar.activation(out=lamt[:, 2:4], in_=lamt[:, 0:2], func=AF.Exp), reads=[lamtR], writes=[lamtR])
            T.op("dve", lambda: nc.vector.tensor_tensor(out=lamt[:, 4:5], in0=lamt[:, 3:4], in1=lamt[:, 2:3], op=ALU.subtract), reads=[lamtR], writes=[lamtR])
            T.op("dve", lambda: nc.vector.tensor_scalar(out=lamt[:, 4:5], in0=lamt[:, 4:5], scalar1=-LAM_INIT0, scalar2=None, op0=ALU.add), reads=[lamtR], writes=[lamtR])
            T.op("act", lambda: nc.scalar.activation(out=silc[:, :, 0], in_=vT[:, :, 21], func=AF.Silu), reads=[vTR], writes=[silcR])
            T.op("act", lambda: nc.scalar.activation(out=silc[:, :, 1], in_=vT[:, :, 20], func=AF.Silu), reads=[vTR], writes=[silcR])
            wab = [sb(ps, "wab%d" % i, [128, 8, 512], BF16) for i in range(3)]
            for jb in range(12):
                wt, wtR = wab[jb % 3]
                mods_dma(0, jb, wt, wtR)
                mods_block(0, jb, wt, wtR)

        T.barrier()
        def CI(l, cd, k):
            return (l * 2 + cd) * 8 + k

        def M(l, s, cd):
            return mods[:, l, s * 8:(s + 1) * 8, cd]

        def derive_cols(l):
            for cd in range(2):
                g1 = vT[:, :, 12 + 4 * l]
                b1 = vT[:, :, 13 + 4 * l]
                g2 = vT[:, :, 14 + 4 * l]
                b2 = vT[:, :, 15 + 4 * l]

                def tt(o, a, b_, op):
                    T.op("dve", lambda: nc.vector.tensor_tensor(out=o, in0=a, in1=b_, op=op), reads=[modsR, vTR, colsR], writes=[colsR])

                def tsc(o, a, s1):
                    T.op("dve", lambda: nc.vector.tensor_scalar(out=o, in0=a, scalar1=s1, scalar2=None, op0=ALU.mult), reads=[vTR, colsR], writes=[colsR])
                tt(cols[:, CI(l, cd, 0), :], g1, M(l, 4, cd), ALU.mult)
                tt(cols[:, CI(l, cd, 1), :], b1, M(l, 4, cd), ALU.mult)
                tt(cols[:, CI(l, cd, 1), :], cols[:, CI(l, cd, 1), :], M(l, 3, cd), ALU.add)
                if l == 0:
                    tsc(cols[:, CI(l, cd, 6), :], g2, ALPHA)
                    tsc(cols[:, CI(l, cd, 7), :], b2, ALPHA)
                else:
                    g20 = vT[:, :, 14]
                    b20 = vT[:, :, 15]
                    tt(cols[:, CI(0, cd, 2), :], g20, M(1, 1, cd), ALU.mult)
                    tt(cols[:, CI(0, cd, 3), :], b20, M(1, 1, cd), ALU.mult)
                    tt(cols[:, CI(0, cd, 3), :], cols[:, CI(0, cd, 3), :], M(1, 0, cd), ALU.add)
                    tsc(cols[:, CI(l, cd, 6), :], g2, 1.0)
                    tsc(cols[:, CI(l, cd, 7), :], b2, 1.0)
                tsc(cols[:, CI(l, cd, 4), :], g1, ALPHA)
                tsc(cols[:, CI(l, cd, 5), :], b1, ALPHA)
        derive_cols(0)
        T.op("dve", lambda: nc.vector.tensor_scalar(out=cols[:, 32, 0:1], in0=vC[:, 0, 2:3], scalar1=(1.0 - LAM_INIT0), scalar2=None, op0=ALU.mult), reads=[vCR], writes=[colsR])
        T.op("dve", lambda: nc.vector.memset(cols[:, 33, :], 1.0), writes=[colsR])

        stage("prologue", [("mods", mods[:].rearrange("p a b c -> p (a b c)"), [128, 192], [modsR]), ("cols", cols[:].rearrange("p a b -> p (a b)"), [128, 320], [colsR]),
                           ("lamt", lamt[:], [128, 8], [lamtR]), ("vT", vT[:].rearrange("p a b -> p (a b)"), [128, 176], [vTR])])

        def ln_fm(stack_tmps, z, zR, C, s0, n, ones, onesR, outs):
            sq, sqR, mean, meanR, rstd, rstdR, tA, tAR = stack_tmps
            bm, bmR = nb()
            bq, bqR = nb()
            F32R = mybir.dt.float32r
            fns = [lambda c=c: nc.tensor.matmul(bm[:, 0:n], lhsT=ones[:], rhs=z[:, c, s0:s0 + n], start=(c == 0), stop=(c == C - 1)) for c in range(C)]
            T.op("pe", fns, reads=[zR, onesR], writes=[bmR])
            for c in range(C):
                T.op("act", lambda c=c: nc.scalar.activation(out=sq[c % 2][:, 0:n], in_=z[:, c, s0:s0 + n], func=AF.Square), reads=[zR], writes=[sqR[c % 2]], big=(n >= 256))
                T.op("pe", lambda c=c: nc.tensor.matmul(bq[:, 0:n], lhsT=ones[:], rhs=sq[c % 2][:, 0:n], start=(c == 0), stop=(c == C - 1)),
                     reads=[sqR[c % 2], onesR], writes=[bqR])
            T.op("act", lambda: nc.scalar.copy(out=mean[:, 0:n], in_=bm[:, 0:n]), reads=[bmR], writes=[meanR], big=(n >= 256))
            T.op("dve", lambda: nc.vector.tensor_tensor(out=rstd[:, 0:n], in0=mean[:, 0:n], in1=mean[:, 0:n], op=ALU.mult), reads=[meanR], writes=[rstdR], big=(n >= 256))
            T.op("dve", lambda: nc.vector.tensor_tensor(out=rstd[:, 0:n], in0=bq[:, 0:n], in1=rstd[:, 0:n], op=ALU.subtract), reads=[bqR, rstdR], writes=[rstdR], big=(n >= 256))
            T.op("dve", lambda: nc.vector.tensor_scalar(out=rstd[:, 0:n], in0=rstd[:, 0:n], scalar1=EPS, scalar2=None, op0=ALU.add), reads=[rstdR], writes=[rstdR], big=(n >= 256))
            T.op("act", lambda: nc.scalar.activation(out=rstd[:, 0:n], in_=rstd[:, 0:n], func=AF.Ln), reads=[rstdR], writes=[rstdR], big=(n >= 256))
            T.op("act", lambda: nc.scalar.activation(out=rstd[:, 0:n], in_=rstd[:, 0:n], func=AF.Exp, scale=-0.5), reads=[rstdR], writes=[rstdR], big=(n >= 256))
            for c in range(C):
                ta, taR = tA[c % 2], tAR[c % 2]
                T.op("dve", lambda c=c, ta=ta: nc.vector.tensor_tensor(out=ta[:, 0:n], in0=z[:, c, s0:s0 + n], in1=mean[:, 0:n], op=ALU.subtract),
                     reads=[zR, meanR], writes=[taR], big=(n >= 256))
                T.op("dve", lambda ta=ta: nc.vector.tensor_tensor(out=ta[:, 0:n], in0=ta[:, 0:n], in1=rstd[:, 0:n], op=ALU.mult),
                     reads=[rstdR, taR], writes=[taR], big=(n >= 256))
                for (eng, fn, rds, wrs) in outs:
                    T.op(eng, lambda fn=fn, c=c, ta=ta: fn(c, ta[:, 0:n]), reads=[taR] + rds, writes=wrs, big=(n >= 256))

        def act_affine(out_ap, in_ap, scale_col, bias_col, func=AF.Identity):
            return nc.scalar.activation(out=out_ap, in_=in_ap, func=func, scale=scale_col, bias=bias_col)

        def dve_affine(out_ap, in_ap, scale_col, bias_col):
            return nc.vector.tensor_scalar(out=out_ap, in0=in_ap, scalar1=scale_col, scalar2=bias_col, op0=ALU.mult, op1=ALU.add)

        with contextlib.ExitStack() as l0:
            lnt = None
            with contextlib.ExitStack() as pj:
                xst[0] = sb(pj, "xst0", [128, 4, D], F32)
                xst[1] = sb(pj, "xst1", [128, 4, D], F32)
                wA, wAR = load_w(pj, "wA", w_att_in_d, 8, 1952)
                wuq, wuqR = load_w(pj, "wuq", w_uq_d, 2, 768)
                wukv, wukvR = load_w(pj, "wukv", w_ukv_d, 1, 1024)
                hk = [sb(pj, "hk%d" % i, [128, 8, 512], BF16) for i in range(2)]
                rk = [sb(pj, "rk%d" % i, [128, 2, 512], F32) for i in range(2)]
                rp = [sb(pj, "rp%d" % i, [32, 2, 512], F32) for i in range(1)]
                kraw = [sb(pj, "kraw%d" % i, [128, 512], BF16) for i in range(1)]
                t1 = [sb(pj, "t1_%d" % i, [128, 512], F32) for i in range(1)]
                t2 = [sb(pj, "t2_%d" % i, [128, 512], F32) for i in range(1)]
                ko = [sb(pj, "ko%d" % i, [128, 512], BF16) for i in range(2)]
                vo = [sb(pj, "vo%d" % i, [128, 8, 128], BF16) for i in range(2)]
                vmo = [sb(pj, "vmo%d" % i, [128, 8, 128], BF16) for i in range(2)]
                of32 = [sb(pj, "of32_%d" % i, [128, 512], F32) for i in range(1)]
                ckv_t = [sb(pj, "ckvt%d" % i, [128, 160], F32) for i in range(2)]
                ckv_all = sb(pj, "ckv_all", [128, 4, 160], F32)
                sq_all = sb(pj, "sq_all", [128, 4, 256], F32)
                cq_all = sb(pj, "cq_all", [128, 4, 256], F32)
                small = [sb(pj, "small%d" % i, [128, 8], F32) for i in range(2)]
                cqn, cqnR = sb(pj, "cqn", [128, 2, 512], BF16)
                ckvn, ckvnR = sb(pj, "ckvn", [128, 1, 512], BF16)
                kvg, kvgR = sb(pj, "kvg", [128, 128], F32)
                T.dma("sp", kvg[:], vecC_d[1:2, 0:128].to_broadcast([128, 128]), writes=[kvgR])
                for i in range(2):
                    T.op("dve", lambda i=i: nc.vector.memset(vo[i][0][:, :, 64:128], 1.0), writes=[vo[i][1]])
                    T.op("dve", lambda i=i: nc.vector.memset(vmo[i][0][:, :, 64:128], 1.0), writes=[vmo[i][1]])
                cnt = {"k": 0}

                def rr(lst):
                    k = cnt.get(id(lst), 0) + 1
                    cnt[id(lst)] = k
                    return lst[k % len(lst)]

                def rope_fm(b, bR, n, rows, perm, permR, tab, tabR, dst, dstR):
                    kr, krR = rr(kraw)
                    T.op("act", lambda: nc.scalar.copy(out=kr[0:rows, 0:n], in_=b[0:rows, 0:n]), reads=[bR], writes=[krR], big=(n >= 256))
                    b2, b2R = nb()
                    T.op("pe", lambda: nc.tensor.matmul(b2[0:rows, 0:n], lhsT=perm[0:rows, 0:rows], rhs=kr[0:rows, 0:n], start=True, stop=True),
                         reads=[krR, permR], writes=[b2R])
                    a1, a1R = rr(t1)
                    a2, a2R = rr(t2)
                    T.op("dve", lambda: nc.vector.tensor_tensor(out=a1[0:rows, 0:n], in0=b[0:rows, 0:n], in1=tab[0:rows, 0, 0:n], op=ALU.mult), reads=[bR, tabR], writes=[a1R], big=(n >= 256))
                    T.op("dve", lambda: nc.vector.tensor_tensor(out=a2[0:rows, 0:n], in0=b2[0:rows, 0:n], in1=tab[0:rows, 1, 0:n], op=ALU.mult), reads=[b2R, tabR], writes=[a2R], big=(n >= 256))
                    T.op("dve", lambda: nc.vector.tensor_tensor(out=dst[0:rows, 0:n], in0=a1[0:rows, 0:n], in1=a2[0:rows, 0:n], op=ALU.add), reads=[a1R, a2R], writes=[dstR], big=(n >= 256))

                def rstd_cols(ss_ap, n_feat, rows, sm, smR, col):
                    T.op("dve", lambda: nc.vector.tensor_scalar(out=sm[0:rows, col:col + 1], in0=ss_ap, scalar1=1.0 / n_feat, scalar2=EPS, op0=ALU.mult, op1=ALU.add), reads=[smR], writes=[smR])
                    T.op("act", lambda: nc.scalar.activation(out=sm[0:rows, col:col + 1], in_=sm[0:rows, col:col + 1], func=AF.Ln), reads=[smR], writes=[smR])
                    T.op("act", lambda: nc.scalar.activation(out=sm[0:rows, col:col + 1], in_=sm[0:rows, col:col + 1], func=AF.Exp, scale=-0.5), reads=[smR], writes=[smR])

                def kv_group(h, hR, s0, n, kc0, is_prompt, tab, tabR, tabp, tabpR, KTd, KTm, Vd, Vm, rKTd, rKTm, rVd, rVm, tok0):
                    nt_ = (n + 127) // 128
                    rw = min(128, n)
                    cta, ctaR = ckv_all

                    def ev_ckv(t, rows, b, bR):
                        T.op("act", lambda: nc.scalar.copy(out=cta[0:rows, t, 0:160], in_=b[0:rows, 0:160]), reads=[bR], writes=[ctaR])

                    def post_ckv():
                        sm, smR = rr(small)
                        a1, a1R = sq_all
                        T.op("dve", lambda: nc.vector.tensor_tensor(out=a1[0:rw, 0:nt_, 0:128], in0=cta[0:rw, 0:nt_, 0:128], in1=cta[0:rw, 0:nt_, 0:128], op=ALU.mult), reads=[ctaR], writes=[a1R], big=True)
                        T.op("dve", lambda: nc.vector.reduce_sum(out=sm[0:rw, 0:nt_], in_=a1[0:rw, 0:nt_, 0:128], axis=AX.X), reads=[a1R], writes=[smR])
                        T.op("dve", lambda: nc.vector.tensor_scalar(out=sm[0:rw, 0:nt_], in0=sm[0:rw, 0:nt_], scalar1=1.0 / 128, scalar2=EPS, op0=ALU.mult, op1=ALU.add), reads=[smR], writes=[smR])
                        T.op("act", lambda: nc.scalar.activation(out=sm[0:rw, 0:nt_], in_=sm[0:rw, 0:nt_], func=AF.Ln), reads=[smR], writes=[smR])
                        T.op("act", lambda: nc.scalar.activation(out=sm[0:rw, 0:nt_], in_=sm[0:rw, 0:nt_], func=AF.Exp, scale=-0.5), reads=[smR], writes=[smR])
                        for t in range(nt_):
                            T.op("dve", lambda t=t: nc.vector.tensor_scalar(out=cta[0:rw, t, 0:128], in0=cta[0:rw, t, 0:128], scalar1=sm[0:rw, t:t + 1], scalar2=None, op0=ALU.mult), reads=[smR, ctaR], writes=[ctaR], big=True)
                        if is_prompt:
                            for t in range(nt_):
                                f, fR = rr(of32)
                                T.op("dve", lambda t=t, f=f: nc.vector.tensor_tensor(out=f[0:rw, 0:128], in0=cta[0:rw, t, 0:128], in1=kvg[0:rw, :], op=ALU.mult), reads=[ctaR, kvgR], writes=[fR], big=True)
                                T.dma("sp", nckv_d[tok0 + t * 128:tok0 + t * 128 + rw, :], f[0:rw, 0:128], reads=[fR], writes=[], is_output=True)
                                T.dma("sp", nkr_d[tok0 + t * 128:tok0 + t * 128 + rw, :], cta[0:rw, t, 128:160], reads=[ctaR], writes=[], is_output=True)

                    def post_ckv2():
                        b2, b2R = nb()
                        fns = [lambda t=t: nc.tensor.transpose(out=b2[:, t * 128:t * 128 + rw], in_=cta[0:rw, t, 0:128], identity=ident[0:rw, 0:rw]) for t in range(nt_)]
                        T.op("pe", fns, reads=[ctaR, identR], writes=[b2R])
                        T.op("act", lambda: nc.scalar.activation(out=ckvn[:, 0, 0:n], in_=b2[:, 0:n], func=AF.Copy, scale=vC[:, 0, 1:2]), reads=[b2R, vCR], writes=[ckvnR], big=True)
                    lin_tm(wA, wAR, 1792, 160, h, hR, s0, n, ev_ckv)
                    post_ckv()
                    def ev_k(j, b, bR):
                        o, oR = rr(ko)
                        if is_prompt:
                            T.op("act", lambda: nc.scalar.copy(out=o[:, 0:n], in_=b[:, 0:n]), reads=[bR], writes=[oR])
                        else:
                            rope_fm(b, bR, n, 128, permd, permdR, tab, tabR, o, oR)
                        T.dma("sp", KTd[j, :, kc0:kc0 + n], o[:, 0:n], reads=[oR], writes=[rKTd])
                    if is_prompt:
                        stage("p1", [("xrP", xrP[:].rearrange("p a b -> p (a b)"), [128, 8 * NP], [xrPR]), ("hP", hP[:].rearrange("p a b -> p (a b)"), [128, 8 * NP], [hPR]),
                                     ("wA", wA[:, 0, 0:512], [128, 512], [wAR])])
                    lin_fm(wA, wAR, 512, 4, h, hR, s0, n, ev_k)
                    if is_prompt:
                        stage("p2")
                    kp, kpR = rr(ko)

                    def ev_kp(j, b, bR):
                        if is_prompt:
                            T.op("act", lambda: nc.scalar.copy(out=kp[0:32, 0:n], in_=b[0:32, 0:n]), reads=[bR], writes=[kpR])
                        else:
                            rope_fm(b, bR, n, 32, permpe, permpeR, tabp, tabpR, kp, kpR)
                        for hh in range(8):
                            T.dma("sp", KTm[hh, 64:96, kc0:kc0 + n], kp[0:32, 0:n], reads=[kpR], writes=[rKTm])
                    lin_fm(wA, wAR, 1920, 1, h, hR, s0, n, ev_kp, m=32)
                    if is_prompt:
                        stage("p3")
                    def ev_v(t, rows, b, bR):
                        o, oR = rr(vo)
                        T.op("act", lambda: nc.scalar.copy(out=o[0:rows, :, 0:64], in_=b[0:rows, 0:512].rearrange("p (h e) -> p h e", h=8)), reads=[bR], writes=[oR])
                        T.dma("sp", Vd[kc0 + t * 128:kc0 + t * 128 + rows, :, :].rearrange("k h (t e) -> k (h t) e", t=2), o[0:rows, :, :], reads=[oR], writes=[rVd])
                        if is_prompt:
                            f, fR = rr(of32)
                            T.op("dve", lambda: nc.vector.tensor_copy(out=f[0:rows, 0:512], in_=b[0:rows, 0:512]), reads=[bR], writes=[fR])
                            T.dma("sp", ndv_d[tok0 + t * 128:tok0 + t * 128 + rows, :], f[0:rows, 0:512], reads=[fR], writes=[], is_output=True)
                    lin_tm(wA, wAR, 1024, 512, h, hR, s0, n, ev_v)
                    if is_prompt:
                        stage("p3a")
                    if is_prompt:
                        def ev_kout(t, rows, b, bR):
                            f, fR = rr(of32)
                            T.op("dve", lambda: nc.vector.tensor_copy(out=f[0:rows, 0:512], in_=b[0:rows, 0:512]), reads=[bR], writes=[fR])
                            T.dma("sp", ndk_d[tok0 + t * 128:tok0 + t * 128 + rows, :], f[0:rows, 0:512], reads=[fR], writes=[], is_output=True)
                        lin_tm(wA, wAR, 512, 512, h, hR, s0, n, ev_kout)

                    post_ckv2()
                    kv_from_ckvn(ckvn, ckvnR, n, kc0, KTm, Vm, rKTm, rVm)

                def kv_from_ckvn(cn, cnR, n, kc0, KTm, Vm, rKTm, rVm):
                    def ev_kn(j, b, bR):
                        o, oR = rr(ko)
                        T.op("act", lambda: nc.scalar.copy(out=o[:, 0:n], in_=b[:, 0:n]), reads=[bR], writes=[oR])
                        T.dma("sp", KTm[j, 0:64, kc0:kc0 + n], o[0:64, 0:n], reads=[oR], writes=[rKTm])
                    lin_fm(wukv, wukvR, 0, 8, cn, cnR, 0, n, ev_kn, K=1)

                    def ev_vm(t, rows, b, bR):
                        pass
                    nt = (n + 127) // 128
                    for t in range(nt):
                        rows = min(128, n - t * 128)
                        o, oR = rr(vmo)
                        for half in range(2):
                            b, bR = nb()
                            T.op("pe", lambda b=b, half=half: nc.tensor.matmul(b[0:rows, 0:512], lhsT=cn[:, 0, t * 128:t * 128 + rows], rhs=wukv[:, 0, half * 512:(half + 1) * 512], start=True, stop=True),
                                 reads=[cnR, wukvR], writes=[bR])
                            T.op("act", lambda b=b, half=half: nc.scalar.copy(out=o[0:rows, half * 4:(half + 1) * 4, 0:64],
                                                                             in_=b[0:rows, 0:512].rearrange("p (h e) -> p h e", h=4)[:, :, 64:128]), reads=[bR], writes=[oR])
                        T.dma("sp", Vm[kc0 + t * 128:kc0 + t * 128 + rows, :, :], o[0:rows, :, :], reads=[oR], writes=[rVm])

                def q_group(h, hR, s0, n, is_prompt, tab, tabR, tabp, tabpR, QTd, QTm, rQTd, rQTm, qc0):
                    def ev_q(j, b, bR):
                        o, oR = rr(ko)
                        T.op("act", lambda: nc.scalar.copy(out=o[:, 0:n], in_=b[:, 0:n]), reads=[bR], writes=[oR])
                        T.dma("sp", QTd[0, j, :, qc0:qc0 + n], o[:, 0:n], reads=[oR], writes=[rQTd])
                        if not is_prompt:
                            o2, o2R = rr(ko)
                            rope_fm(b, bR, n, 128, permd, permdR, tab, tabR, o2, o2R)
                            T.dma("sp", QTd[1, j, :, qc0:qc0 + n], o2[:, 0:n], reads=[o2R], writes=[rQTd])
                    lin_fm(wA, wAR, 0, 4, h, hR, s0, n, ev_q)

                    nt_ = (n + 127) // 128
                    rw = min(128, n)
                    cqa, cqaR = cq_all

                    def ev_cq(t, rows, b, bR):
                        T.op("act", lambda: nc.scalar.copy(out=cqa[0:rows, t, :], in_=b[0:rows, 0:256]), reads=[bR], writes=[cqaR])

                    def post_cq():
                        sm, smR = rr(small)
                        a1, a1R = sq_all
                        T.op("dve", lambda: nc.vector.tensor_tensor(out=a1[0:rw, 0:nt_, :], in0=cqa[0:rw, 0:nt_, :], in1=cqa[0:rw, 0:nt_, :], op=ALU.mult), reads=[cqaR], writes=[a1R], big=True)
                        T.op("dve", lambda: nc.vector.reduce_sum(out=sm[0:rw, 0:nt_], in_=a1[0:rw, 0:nt_, :], axis=AX.X), reads=[a1R], writes=[smR])
                        T.op("dve", lambda: nc.vector.tensor_scalar(out=sm[0:rw, 0:nt_], in0=sm[0:rw, 0:nt_], scalar1=1.0 / 256, scalar2=EPS, op0=ALU.mult, op1=ALU.add), reads=[smR], writes=[smR])
                        T.op("act", lambda: nc.scalar.activation(out=sm[0:rw, 0:nt_], in_=sm[0:rw, 0:nt_], func=AF.Ln), reads=[smR], writes=[smR])
                        T.op("act", lambda: nc.scalar.activation(out=sm[0:rw, 0:nt_], in_=sm[0:rw, 0:nt_], func=AF.Exp, scale=-0.5), reads=[smR], writes=[smR])
                        for t in range(nt_):
                            T.op("dve", lambda t=t: nc.vector.tensor_scalar(out=cqa[0:rw, t, :], in0=cqa[0:rw, t, :], scalar1=sm[0:rw, t:t + 1], scalar2=None, op0=ALU.mult), reads=[smR, cqaR], writes=[cqaR], big=True)
                        for c2 in range(2):
                            b2, b2R = nb()
                            fns = [lambda t=t, c2=c2, b2=b2: nc.tensor.transpose(out=b2[:, t * 128:t * 128 + rw], in_=cqa[0:rw, t, c2 * 128:(c2 + 1) * 128], identity=ident[0:rw, 0:rw]) for t in range(nt_)]
                            T.op("pe", fns, reads=[cqaR, identR], writes=[b2R])
                            T.op("act", lambda b2=b2, c2=c2: nc.scalar.activation(out=cqn[:, c2, 0:n], in_=b2[:, 0:n], func=AF.Copy, scale=vC[:, c2, 0:1]), reads=[b2R, vCR], writes=[cqnR], big=True)
                    lin_tm(wA, wAR, 1536, 256, h, hR, s0, n, ev_cq)
                    post_cq()
                    for hh in range(8):
                        def ev_qn(j, b, bR, hh=hh):
                            o, oR = rr(ko)
                            T.op("act", lambda: nc.scalar.copy(out=o[0:64, 0:n], in_=b[0:64, 0:n]), reads=[bR], writes=[oR])
                            for v in range(1 if is_prompt else 2):
                                T.dma("sp", QTm[v, hh, 0:64, qc0:qc0 + n], o[0:64, 0:n], reads=[oR], writes=[rQTm])
                        lin_fm(wuq, wuqR, hh * 96, 1, cqn, cqnR, 0, n, ev_qn, K=2, m=64)

                        def ev_qp(j, b, bR, hh=hh):
                            o, oR = rr(ko)
                            T.op("act", lambda: nc.scalar.copy(out=o[0:32, 0:n], in_=b[0:32, 0:n]), reads=[bR], writes=[oR])
                            T.dma("sp", QTm[0, hh, 64:96, qc0:qc0 + n], o[0:32, 0:n], reads=[oR], writes=[rQTm])
                            if not is_prompt:
                                o2, o2R = rr(ko)
                                rope_fm(b, bR, n, 32, permpe, permpeR, tabp, tabpR, o2, o2R)
                                T.dma("sp", QTm[1, hh, 64:96, qc0:qc0 + n], o2[0:32, 0:n], reads=[o2R], writes=[rQTm])
                        lin_fm(wuq, wuqR, hh * 96 + 64, 1, cqn, cqnR, 0, n, ev_qp, K=2, m=32)

                def mod_evac(hdst, hdstR, xr, xrR, s0, n, l, cd):
                    def ev(c, b, bR):
                        T.op("act", lambda: act_affine(hdst[:, c, s0:s0 + n], b[:, 0:n], mods[:, l, 8 + c:9 + c, cd], mods[:, l, c:c + 1, cd]),
                             reads=[bR, modsR], writes=[hdstR], big=True)
                        if xr is not None:
                            T.op("dve", lambda: nc.vector.tensor_scalar(out=xr[:, c, s0:s0 + n], in0=b[:, 0:n], scalar1=ALPHA, scalar2=None, op0=ALU.mult),
                                 reads=[bR], writes=[xrR], big=True)
                    return ev

                load_T(xp_d[:, :], NP, mod_evac(hP, hPR, xrP, xrPR, 0, NP, 0, 0))
                kv_group(hP, hPR, 0, NP, 0, True, None, None, None, None, KTd_P, KTm_P, Vd_P, Vm_P, R["KTd_P"], R["KTm_P"], R["Vd_P"], R["Vm_P"], 0)
                q_group(hP, hPR, 0, NP, True, None, None, None, None, QTd_P, QTm_P, R["QTd_P"], R["QTm_P"], 0)

                stage("projP", [("hP", None, None, None)] if False else [])
                for t in range(2):
                    ct, ctR = rr(of32)
                    T.dma("sp", ct[:, 0:512], ck_d[t * 128:(t + 1) * 128, :], writes=[ctR])
                    for hh in range(4):
                        b, bR = nb()
                        T.op("pe", lambda b=b, hh=hh, ct=ct: nc.tensor.transpose(out=b[:, 0:128], in_=ct[:, hh * 128:(hh + 1) * 128], identity=ident[:, :]), reads=[ctR, identR], writes=[bR])
                        o, oR = rr(ko)
                        T.op("act", lambda b=b, o=o: nc.scalar.copy(out=o[:, 0:128], in_=b[:, 0:128]), reads=[bR], writes=[oR])
                        T.dma("sp", KTd_S[hh, :, t * 128:(t + 1) * 128], o[:, 0:128], reads=[oR], writes=[R["KTd_S"]])
                    cvt, cvtR = rr(of32)
                    T.dma("sp", cvt[:, 0:512], cv_d[t * 128:(t + 1) * 128, :], writes=[cvtR])
                    o, oR = rr(vo)
                    T.op("act", lambda o=o, cvt=cvt: nc.scalar.copy(out=o[:, :, 0:64], in_=cvt[:, 0:512].rearrange("p (h e) -> p h e", h=8)), reads=[cvtR], writes=[oR])
                    T.dma("sp", Vd_S[t * 128:(t + 1) * 128, :, :].rearrange("k h (t e) -> k (h t) e", t=2), o[:, :, :], reads=[oR], writes=[R["Vd_S"]])
                    c2, c2R = rr(ckv_t)
                    T.dma("sp", c2[:, 0:128], cckv_d[t * 128:(t + 1) * 128, :], writes=[c2R])
                    T.dma("sp", c2[:, 128:160], ckr_d[t * 128:(t + 1) * 128, :], writes=[c2R])
                    b, bR = nb()
                    T.op("pe", lambda b=b, c2=c2: nc.tensor.transpose(out=b[:, 0:128], in_=c2[:, 0:128], identity=ident[:, :]), reads=[c2R, identR], writes=[bR])
                    T.op("act", lambda b=b, t=t: nc.scalar.copy(out=ckvn[:, 0, t * 128:(t + 1) * 128], in_=b[:, 0:128]), reads=[bR], writes=[ckvnR])
                    b, bR = nb()
                    T.op("pe", lambda b=b, c2=c2: nc.tensor.transpose(out=b[0:32, 0:128], in_=c2[:, 128:160], identity=ident[:, :]), reads=[c2R, identR], writes=[bR])
                    o, oR = rr(ko)
                    T.op("act", lambda b=b, o=o: nc.scalar.copy(out=o[0:32, 0:128], in_=b[0:32, 0:128]), reads=[bR], writes=[oR])
                    for hh in range(8):
                        T.dma("sp", KTm_S[hh, 64:96, t * 128:(t + 1) * 128], o[0:32, 0:128], reads=[oR], writes=[R["KTm_S"]])
                kv_from_ckvn(ckvn, ckvnR, 256, 0, KTm_S, Vm_S, R["KTm_S"], R["Vm_S"])

                for g in range(TS // 512):
                    hh_, hhR = hk[g % 2]
                    tab, tabR = rk[g % 2]
                    tabp, tabpR = rp[0]
                    T.dma("pool", tab[:, :, :], ropek_d[:, :, g * 512:(g + 1) * 512].rearrange("a p t -> p a t"), writes=[tabR])
                    T.dma("pool", tabp[:, :, :], ropekpe_d[:, :, g * 512:(g + 1) * 512].rearrange("a p t -> p a t"), writes=[tabpR])
                    load_T(xkv_d[g * 512:(g + 1) * 512, :], 512, mod_evac(hh_, hhR, None, None, 0, 512, 0, 1))
                    kv_group(hh_, hhR, 0, 512, PAST + g * 512, False, tab, tabR, tabp, tabpR, KTd_S, KTm_S, Vd_S, Vm_S,
                             R["KTd_S"], R["KTm_S"], R["Vd_S"], R["Vm_S"], 0)
                for (s0, n) in ((0, 352), (352, 352), (704, 352)):
                    tab, tabR = rr(rk)
                    tabp, tabpR = rr(rp)
                    T.dma("pool", tab[:, :, 0:n], ropeq_d[:, :, s0:s0 + n].rearrange("a p t -> p a t"), writes=[tabR])
                    T.dma("pool", tabp[:, :, 0:n], ropeqpe_d[:, :, s0:s0 + n].rearrange("a p t -> p a t"), writes=[tabpR])
                    load_T(xq_d[s0:s0 + n, :], n, mod_evac(hS, hSR, xrS, xrSR, s0, n, 0, 1))
                    q_group(hS, hSR, s0, n, False, tab, tabR, tabp, tabpR, QTd_S, QTm_S, R["QTd_S"], R["QTm_S"], s0)

            T.barrier()
            stage("proj")
            with contextlib.ExitStack() as at:
                Kb = [sb(at, "Kb%d" % i, [128, NKS], BF16) for i in range(2)]
                Vb = [sb(at, "Vb%d" % i, [128, 34, 256], BF16) for i in range(2)]
                Qb = [sb(at, "Qb%d" % i, [128, 2, 2, NQ], BF16) for i in range(2)]
                for i in range(2):
                    T.op("dve", lambda i=i: nc.vector.memset(Qb[i][0][:, :, :, :], 0.0), writes=[Qb[i][1]])
                Qm = [sb(at, "Qm%d" % i, [128, 2, NQ], BF16) for i in range(2)]
                PT = [sb(at, "PT%d" % i, [128, 512], BF16) for i in range(4)]
                ofm = [sb(at, "ofm%d" % i, [128, 2, 512], F32) for i in range(2)]
                otm = [sb(at, "otm%d" % i, [128, 4, 256], F32) for i in range(4)]
                og = [sb(at, "og_%d" % i, [128, 4, 128], F32) for i in range(3)]
                sm2 = [sb(at, "sm2_%d" % i, [128, 8], F32) for i in range(4)]
                sqt = [sb(at, "sqt%d" % i, [128, 128], F32) for i in range(2)]
                cnt2 = {}

                def rr2(lst):
                    k = cnt2.get(id(lst), 0) + 1
                    cnt2[id(lst)] = k
                    return lst[k % len(lst)]
                set_rot([4, 5, 6, 7])
                accb = banks[0:4]
                acc_i = [0]
                deferred = []
                wab1 = [sb(at, "wab1_%d" % i, [128, 8, 256], BF16) for i in range(4)]
                mods1 = {"dma": 0, "mm": 0}

                def mods1_mm():
                    while mods1["mm"] < mods1["dma"]:
                        jb = mods1["mm"]
                        mods_block(1, jb, *wab1[jb % 4], nck=2)
                        mods1["mm"] += 1

                def mods1_dma():
                    for _ in range(4):
                        if mods1["dma"] < 24:
                            jb = mods1["dma"]
                            mods_dma(1, jb, *wab1[jb % 4], nck=2)
                            mods1["dma"] += 1

                def mods1_step():
                    mods1_mm()
                    mods1_dma()
                for i in range(len(sm2)):
                    T.op("dve", lambda i=i: nc.vector.memset(sm2[i][0][:, :], 1.0), writes=[sm2[i][1]])

                def attn_qgroup(K_, KR, prow, Q_, QR, q0, n, V_, VR, voffs, nch, nctx, scale, accs, qmap=None):
                    def qk(ck):
                        b, bR = nb()
                        v = 0 if ck < nctx else 1
                        rhs = Q_[prow, v, q0:q0 + n] if qmap is None else Q_[prow, v, qmap, q0:q0 + n]
                        T.op("pe", lambda: nc.tensor.matmul(b[:, 0:n], lhsT=K_[prow, ck * 128:(ck + 1) * 128], rhs=rhs, start=True, stop=True),
                             reads=[KR, QR], writes=[bR])
                        return b, bR
                    LA = 2
                    pend = [qk(i) for i in range(min(LA, nch))]
                    for ck in range(nch):
                        b, bR = pend.pop(0)
                        pt, ptR = rr2(PT)
                        T.op("act", lambda b=b, pt=pt: nc.scalar.activation(out=pt[:, 0:n], in_=b[:, 0:n], func=AF.Exp, scale=scale), reads=[bR], writes=[ptR], big=(n >= 256))
                        if ck + LA < nch:
                            pend.append(qk(ck + LA))
                        if ck == min(6, nch - 1):
                            while deferred:
                                deferred.pop(0)()
                        fns = []
                        for pi, off in enumerate(voffs):
                            fns.append(lambda pi=pi, off=off, pt=pt, ck=ck: nc.tensor.matmul(accs[pi][0][:, 0:n], lhsT=V_[:, ck, off:off + 128], rhs=pt[:, 0:n],
                                                                                            start=(ck == 0), stop=(ck == nch - 1)))
                        T.op("pe", fns, reads=[ptR, VR], writes=[a[1] for a in accs])

                def evac_tm(accs, n, nparts):
                    nsub = (n + 127) // 128
                    of_, ofR = rr2(ofm)
                    for pi in range(nparts):
                        a, aR = accs[pi]
                        T.op("dve", lambda a=a, pi=pi: nc.vector.tensor_copy(out=of_[:, pi, 0:n], in_=a[:, 0:n]), reads=[aR], writes=[ofR], big=(n >= 256))
                    ot_, otR = rr2(otm)
                    W = nparts * 128
                    per = 512 // W
                    for s0 in range(0, nsub, per):
                        ss = list(range(s0, min(nsub, s0 + per)))
                        tb, tbR = nb()
                        fns = []
                        for j, s in enumerate(ss):
                            qs = min(128, n - s * 128)
                            for pi in range(nparts):
                                fns.append(lambda j=j, s=s, qs=qs, pi=pi: nc.tensor.transpose(out=tb[0:qs, j * W + pi * 128:j * W + (pi + 1) * 128],
                                                                                             in_=of_[:, pi, s * 128:s * 128 + qs], identity=ident[:, :]))
                        T.op("pe", fns, reads=[ofR, identR], writes=[tbR])
                        qs0 = min(128, n - ss[0] * 128)
                        T.op("dve", lambda ss=ss, qs0=qs0: nc.vector.tensor_copy(out=ot_[0:qs0, ss[0]:ss[-1] + 1, 0:W],
                                                                             in_=tb[0:qs0, 0:len(ss) * W].rearrange("p (s w) -> p s w", w=W)), reads=[tbR], writes=[otR])
                    return ot_, otR

                def run_attention(name, QTd, QTm, KTd, KTm, Vd, Vm, cat, catR, qgroups, kcol0, nk, nctx, nvar, nqtot, qcol0):
                    nch = nk // 128
                    nctx_ = nctx if nvar == 2 else nch + 1
                    for hh in range(4):
                        K_, KR = rr2(Kb)
                        V_, VR = rr2(Vb)
                        Q_, QR = rr2(Qb)
                        T.dma("pool", K_[:, 0:nk], KTd[hh, :, kcol0:kcol0 + nk], writes=[KR])
                        T.dma("pool", V_[:, 0:nch, :], Vd[kcol0:kcol0 + nk, hh, :].rearrange("(c p) e -> p c e", p=128), writes=[VR])
                        for v in range(nvar):
                            T.dma("pool", Q_[0:64, v, 0, 0:nqtot], QTd[v, hh, 0:64, qcol0:qcol0 + nqtot], writes=[QR])
                            T.dma("pool", Q_[64:128, v, 1, 0:nqtot], QTd[v, hh, 64:128, qcol0:qcol0 + nqtot], writes=[QR])
                        for (q0, n) in qgroups:
                            nsub = (n + 127) // 128
                            ogt, ogR = rr2(og)
                            oa = []
                            for m in range(2):
                                accs = [accb[2 * m], accb[2 * m + 1]]
                                attn_qgroup(K_, KR, slice(0, 128), Q_, QR, q0, n, V_, VR, [0, 128], nch, nctx_, 0.125, accs, qmap=m)
                                oa.append(evac_tm(accs, n, 2))
                            (oa0, oa0R), (oa1, oa1R) = oa
                            sm, smR = rr2(sm2)
                            for s in range(nsub):
                                qs = min(128, n - s * 128)
                                v0 = oa0[0:qs, s, :].rearrange("p (t e) -> p t e", t=2)[:, :, 0:64]
                                v1 = oa1[0:qs, s, :].rearrange("p (t e) -> p t e", t=2)[:, :, 0:64]
                                ov = ogt[0:qs, s, :].rearrange("p (t e) -> p t e", t=2)
                                T.op("dve", lambda: nc.vector.reciprocal(out=sm[0:qs, 0:1], in_=oa0[0:qs, s, 64:65]), reads=[oa0R], writes=[smR])
                                T.op("dve", lambda: nc.vector.reciprocal(out=sm[0:qs, 1:2], in_=oa1[0:qs, s, 64:65]), reads=[oa1R], writes=[smR])
                                T.op("dve", lambda: nc.vector.tensor_tensor(out=sm[0:qs, 1:2], in0=sm[0:qs, 1:2], in1=lamt[0:qs, 4:5], op=ALU.mult), reads=[smR, lamtR], writes=[smR])
                                T.op("dve", lambda: nc.vector.tensor_scalar(out=v0, in0=v0, scalar1=sm[0:qs, 0:1], scalar2=None, op0=ALU.mult),
                                     reads=[smR, oa0R], writes=[oa0R])
                                T.op("dve", lambda: nc.vector.scalar_tensor_tensor(out=ov, in0=v1, scalar=sm[0:qs, 1:2], in1=v0,
                                                                                  op0=ALU.mult, op1=ALU.add), reads=[smR, oa0R, oa1R], writes=[ogR])
                                sq_, sqR_ = rr2(sqt)
                                T.op("dve", lambda: nc.vector.tensor_tensor(out=sq_[0:qs, :], in0=ogt[0:qs, s, :], in1=ogt[0:qs, s, :], op=ALU.mult), reads=[ogR], writes=[sqR_])
                                T.op("dve", lambda: nc.vector.reduce_sum(out=sm[0:qs, 4 + s:5 + s], in_=sq_[0:qs, :], axis=AX.X), reads=[sqR_], writes=[smR])
                            T.op("dve", lambda: nc.vector.tensor_scalar(out=sm[:, 4:4 + nsub], in0=sm[:, 4:4 + nsub], scalar1=1.0 / 128, scalar2=EPS, op0=ALU.mult, op1=ALU.add), reads=[smR], writes=[smR])

                            def fin(sm=sm, smR=smR, ogt=ogt, ogR=ogR, n=n, nsub=nsub, q0=q0, hh=hh):
                                T.op("act", lambda: nc.scalar.activation(out=sm[:, 4:4 + nsub], in_=sm[:, 4:4 + nsub], func=AF.Ln), reads=[smR], writes=[smR])
                                T.op("act", lambda: nc.scalar.activation(out=sm[:, 4:4 + nsub], in_=sm[:, 4:4 + nsub], func=AF.Exp, scale=-0.5), reads=[smR], writes=[smR])
                                for s in range(nsub):
                                    qs = min(128, n - s * 128)
                                    T.op("dve", lambda: nc.vector.tensor_scalar(out=ogt[0:qs, s, :], in0=ogt[0:qs, s, :], scalar1=sm[0:qs, 4 + s:5 + s], scalar2=None, op0=ALU.mult), reads=[smR, ogR], writes=[ogR])
                                nf = n // 128
                                r0 = qcol0 + q0
                                if nf > 0:
                                    T.dma("sp", cat[r0:r0 + nf * 128, hh * 128:(hh + 1) * 128].rearrange("(s p) e -> p s e", p=128), ogt[:, 0:nf, :], reads=[ogR], writes=[catR])
                                if n % 128:
                                    T.dma("sp", cat[r0 + nf * 128:r0 + n, hh * 128:(hh + 1) * 128], ogt[0:n - nf * 128, nf, :], reads=[ogR], writes=[catR])
                            deferred.append(fin)
                    while deferred:
                        deferred.pop(0)()
                    for hh in range(8):
                        if nvar == 2:
                            mods1_mm()
                        K_, KR = rr2(Kb)
                        V_, VR = rr2(Vb)
                        Q_, QR = rr2(Qm)
                        T.dma("pool", K_[0:96, 0:nk], KTm[hh, :, kcol0:kcol0 + nk], writes=[KR])
                        T.dma("pool", V_[:, 0:nch, 0:128], Vm[kcol0:kcol0 + nk, hh, :].rearrange("(c p) e -> p c e", p=128), writes=[VR])
                        for v in range(nvar):
                            T.dma("pool", Q_[0:96, v, 0:nqtot], QTm[v, hh, :, qcol0:qcol0 + nqtot], writes=[QR])
                        if nvar == 2:
                            mods1_dma()
                        for (q0, n) in qgroups:
                            nsub = (n + 127) // 128
                            ogt, ogR = rr2(og)
                            acc_i[0] = (acc_i[0] + 1) % 4
                            accs = [accb[acc_i[0]]]
                            attn_qgroup(K_, KR, slice(0, 96), Q_, QR, q0, n, V_, VR, [0], nch, nctx_, 96.0 ** -0.5, accs)
                            oat, oaR = evac_tm(accs, n, 1)
                            for s in range(nsub):
                                qs = min(128, n - s * 128)
                                sm, smR = rr2(sm2)
                                T.op("dve", lambda: nc.vector.reciprocal(out=sm[0:qs, 0:1], in_=oat[0:qs, s, 64:65]), reads=[oaR], writes=[smR])
                                T.op("dve", lambda: nc.vector.tensor_scalar(out=ogt[0:qs, s, 0:64], in0=oat[0:qs, s, 0:64], scalar1=sm[0:qs, 0:1], scalar2=None, op0=ALU.mult),
                                     reads=[oaR, smR], writes=[ogR])
                            c0 = 512 + hh * 64
                            nf = n // 128
                            r0 = qcol0 + q0
                            if nf > 0:
                                T.dma("sp", cat[r0:r0 + nf * 128, c0:c0 + 64].rearrange("(s p) e -> p s e", p=128), ogt[:, 0:nf, 0:64], reads=[ogR], writes=[catR])
                            if n % 128:
                                T.dma("sp", cat[r0 + nf * 128:r0 + n, c0:c0 + 64], ogt[0:n - nf * 128, nf, 0:64], reads=[ogR], writes=[catR])
                namesP = {k: k + "_P" for k in ("QTd", "QTm", "KTd", "KTm", "Vd", "Vm")}
                namesS = {k: k + "_S" for k in ("QTd", "QTm", "KTd", "KTm", "Vd", "Vm")}
                for sidx in range(2):
                    run_attention(namesP, QTd_P, QTm_P, KTd_P, KTm_P, Vd_P, Vm_P, cat_P, R["cat_P"], [(0, 256)], sidx * 256, 256, 0, 1, 256, sidx * 256)
                run_attention(namesS, QTd_S, QTm_S, KTd_S, KTm_S, Vd_S, Vm_S, cat_S, R["cat_S"], [(0, 352), (352, 352), (704, 352)], 0, NKS, 2, 2, NQ, 0)
                while mods1["mm"] < 24:
                    mods1_step()
                set_rot(list(range(8)))

            T.barrier()
            derive_cols(1)
            stage("attn")
            groupsP = [(0, 512)]
            groupsS0 = [(0, 352), (352, 352), (704, 352)]
            with contextlib.ExitStack() as op_:
                xst[0] = sb(op_, "xst0b", [128, 4, D], F32)
                xst[1] = sb(op_, "xst1b", [128, 4, D], F32)
                wo, woR = load_w(op_, "wo", w_att_out_d, 8, D)
                catT, catTR = sb(op_, "catT", [128, 8, 512], BF16)
                lnt = mk_ln_tmps(op_) if False else None
                sq = [sb(op_, "lsq%d" % i, [128, 512], F32) for i in range(2)]
                tA = [sb(op_, "ltA%d" % i, [128, 512], F32) for i in range(2)]
                mean, meanR = sb(op_, "lmean", [128, 512], F32)
                rstd, rstdR = sb(op_, "lrstd", [128, 512], F32)
                lnt = ([s[0] for s in sq], [s[1] for s in sq], mean, meanR, rstd, rstdR, [s[0] for s in tA], [s[1] for s in tA])

                pend_ln = []

                def mixer_out(cat, catR, xr, xrR, hdst, hdstR, groups, l, cd, W, WR):
                    for (s0, n) in groups:
                        def ev_cat(c, b, bR):
                            sc = cols[:, 32, 0:1] if (l == 0 and c < 4) else cols[:, 33, 0:1]
                            T.op("act", lambda: nc.scalar.activation(out=catT[:, c, 0:n], in_=b[:, 0:n], func=AF.Copy, scale=sc), reads=[bR, colsR], writes=[catTR])
                        load_T(cat[s0:s0 + n, :], n, ev_cat)

                        def ev_y(j, b, bR):
                            T.op("dve", lambda: nc.vector.scalar_tensor_tensor(out=xr[:, j, s0:s0 + n], in0=b[:, 0:n], scalar=mods[:, l, 16 + j:17 + j, cd], in1=xr[:, j, s0:s0 + n],
                                                                              op0=ALU.mult, op1=ALU.add), reads=[bR, modsR, xrR], writes=[xrR], big=(n >= 256))
                        lin_fm(W, WR, 0, 8, catT, catTR, 0, n, ev_y)
                        while pend_ln:
                            ln1(*pend_ln.pop(0))
                        pend_ln.append((xr, xrR, hdst, hdstR, s0, n, l, cd))

                def ln1(xr, xrR, hdst, hdstR, s0, n, l, cd):
                    outs = [
                        ("act", lambda c, xn: act_affine(hdst[:, c, s0:s0 + n], xn, cols[:, CI(l, cd, 0), c:c + 1], cols[:, CI(l, cd, 1), c:c + 1]), [colsR], [hdstR]),
                        ("dve", lambda c, xn: dve_affine(xr[:, c, s0:s0 + n], xn, cols[:, CI(l, cd, 4), c:c + 1], cols[:, CI(l, cd, 5), c:c + 1]), [colsR], [xrR]),
                    ]
                    ln_fm(lnt, xr, xrR, 8, s0, n, onesD, onesDR, outs)

                mixer_out(cat_P, R["cat_P"], xrP, xrPR, hP, hPR, groupsP, 0, 0, wo, woR)
                mixer_out(cat_S, R["cat_S"], xrS, xrSR, hS, hSR, groupsS0, 0, 1, wo, woR)
                while pend_ln:
                    ln1(*pend_ln.pop(0))

        T.barrier()
        stage("mix0", [("xrP", xrP[:].rearrange("p a b -> p (a b)"), [128, 8 * NP], [xrPR]), ("xrS", xrS[:].rearrange("p a b -> p (a b)"), [128, 8 * NQ], [xrSR])])
        def mlp_layer(l, work, final):
            with contextlib.ExitStack() as ms:
                w1 = [sb(ms, "w1_%d" % i, [128, 8, 1024], BF16) for i in range(2)]
                w2 = [sb(ms, "w2_%d" % i, [128, 8, 1024], BF16) for i in range(2)]
                hid = [sb(ms, "hid%d" % i, [128, 8, 512], BF16) for i in range(2)]
                rl = [sb(ms, "rl%d" % i, [128, 512], F32) for i in range(2)]
                sq = [sb(ms, "msq%d" % i, [128, 512], F32) for i in range(2)]
                tA = [sb(ms, "mtA%d" % i, [128, 512], F32) for i in range(2)]
                mean, meanR = sb(ms, "mmean", [128, 512], F32)
                rstd, rstdR = sb(ms, "mrstd", [128, 512], F32)
                lnt2 = ([s[0] for s in sq], [s[1] for s in sq], mean, meanR, rstd, rstdR, [s[0] for s in tA], [s[1] for s in tA])
                k = 0
                for q in range(4):
                    wa, waR = w1[q % 2]
                    wb, wbR = w2[q % 2]
                    T.dma("pool", wa[:, :, :], w_mlp_in_d[l].rearrange("(c p) n -> p c n", p=128)[:, :, q * 1024:(q + 1) * 1024], writes=[waR])
                    T.dma("pool", wb[:, :, :], w_mlp_out_d[l, q * 1024:(q + 1) * 1024, :].rearrange("(c p) n -> p c n", p=128), writes=[wbR])
                    for (xr, xrR, h, hR, s0, n, cd) in work:
                        hd, hdR = hid[k % 2]
                        k += 1

                        def ev_h(j, b, bR):
                            r_, rR_ = rl[j % 2]
                            T.op("act", lambda: nc.scalar.activation(out=r_[:, 0:n], in_=b[:, 0:n], func=AF.Relu), reads=[bR], writes=[rR_], big=(n >= 256))
                            T.op("dve", lambda: nc.vector.tensor_tensor(out=hd[:, j, 0:n], in0=b[:, 0:n], in1=r_[:, 0:n], op=ALU.mult), reads=[bR, rR_], writes=[hdR], big=(n >= 256))
                        lin_fm(wa, waR, 0, 8, h, hR, s0, n, ev_h)

                        def ev_y(j, b, bR):
                            T.op("dve", lambda: nc.vector.scalar_tensor_tensor(out=xr[:, j, s0:s0 + n], in0=b[:, 0:n], scalar=mods[:, l, 40 + j:41 + j, cd], in1=xr[:, j, s0:s0 + n],
                                                                              op0=ALU.mult, op1=ALU.add), reads=[bR, modsR, xrR], writes=[xrR], big=(n >= 256))
                        lin_fm(wb, wbR, 0, 8, hd, hdR, 0, n, ev_y)
                for (xr, xrR, h, hR, s0, n, cd) in work:
                    outs = [("dve", lambda c, xn, xr=xr, s0=s0, n=n, cd=cd: dve_affine(xr[:, c, s0:s0 + n], xn, cols[:, CI(l, cd, 6), c:c + 1], cols[:, CI(l, cd, 7), c:c + 1]), [colsR], [xrR])]
                    if not final:
                        outs.append(("act", lambda c, xn, h=h, s0=s0, n=n, cd=cd: act_affine(h[:, c, s0:s0 + n], xn, cols[:, CI(l, cd, 2), c:c + 1], cols[:, CI(l, cd, 3), c:c + 1]), [colsR], [hR]))
                    ln_fm(lnt2, xr, xrR, 8, s0, n, onesD, onesDR, outs)

        mlp_layer(0, [(xrP, xrPR, hP, hPR, 0, 512, 0), (xrS, xrSR, hS, hSR, 0, 352, 1), (xrS, xrSR, hS, hSR, 352, 352, 1), (xrS, xrSR, hS, hSR, 704, 352, 1)], False)

        T.barrier()
        stage("mlp0", [("xrP", xrP[:].rearrange("p a b -> p (a b)"), [128, 8 * NP], [xrPR]), ("xrS", xrS[:].rearrange("p a b -> p (a b)"), [128, 8 * NQ], [xrSR])])
        with contextlib.ExitStack() as l1:
            wC, wCR = load_w(l1, "wC", w_conv_in_d, 8, 1536)
            wco, wcoR = load_w(l1, "wco", w_conv_out_d, 8, D)
            wpl, wplR = sb(l1, "wpl", [128, 4, 128], BF16)
            T.dma("pool", wpl[:, :, :], w_pool_d.rearrange("g c e -> c g e"), writes=[wplR])
            LB = NQ + 32
            ub, ubR = sb(l1, "ub", [128, 4, LB], BF16)
            pzb = [sb(l1, "pzb%d" % i, [128, LB], F32) for i in range(1)]
            Ta, TaR = sb(l1, "Ta", [128, LB], F32)
            Tb, TbR = sb(l1, "Tb", [128, LB], F32)
            ict = [sb(l1, "ict%d" % i, [128, NOWN], F32) for i in range(1)]
            sg = [sb(l1, "sg%d" % i, [128, 512], F32) for i in range(1)]
            dgs = [sb(l1, "dg%d" % i, [128, CONVW, 128], BF16) for i in range(2)]
            dtl = [sb(l1, "dtl%d" % i, [128, 512], BF16) for i in range(1)]
            c1T, c1TR = sb(l1, "c1T", [128, 8, NOWN], BF16)
            sq = [sb(l1, "csq%d" % i, [128, 512], F32) for i in range(2)]
            tA = [sb(l1, "ctA%d" % i, [128, 512], F32) for i in range(2)]
            mean, meanR = sb(l1, "cmean", [128, 512], F32)
            rstd, rstdR = sb(l1, "crstd", [128, 512], F32)
            lnt3 = ([s[0] for s in sq], [s[1] for s in sq], mean, meanR, rstd, rstdR, [s[0] for s in tA], [s[1] for s in tA])

            def conv_pool(h, hR, xr, xrR, cd, in_groups, segs, ic_d, ntot, is_sample):
                boff = in_groups[0][2]
                T.op("dve", lambda: nc.vector.memset(ub[:, :, :], 0.0), writes=[ubR])
                for c in range(4):
                    for (s0, n, bo) in in_groups:
                        def ev_gate(j, b, bR):
                            s_, sR_ = sg[0]
                            T.op("act", lambda: nc.scalar.activation(out=s_[:, 0:n], in_=b[:, 0:n], func=AF.Sigmoid), reads=[bR], writes=[sR_])
                        lin_fm(wC, wCR, 512 + c * 128, 1, h, hR, s0, n, ev_gate)

                        def ev_a(j, b, bR):
                            s_, sR_ = sg[0]
                            T.op("dve", lambda: nc.vector.tensor_tensor(out=ub[:, c, bo:bo + n], in0=b[:, 0:n], in1=s_[:, 0:n], op=ALU.mult), reads=[bR, sR_], writes=[ubR])
                        lin_fm(wC, wCR, c * 128, 1, h, hR, s0, n, ev_a)
                if is_sample:
                    for c in range(4):
                        T.op("dve", lambda c=c: nc.vector.tensor_tensor(out=ub[:, c, 0:16], in0=ub[:, c, 0:16], in1=hmask[:, 0:16], op=ALU.mult), reads=[hmaskR, ubR], writes=[ubR])
                        T.op("dve", lambda c=c: nc.vector.tensor_tensor(out=ub[:, c, 16 + NOWN:32 + NOWN], in0=ub[:, c, 16 + NOWN:32 + NOWN], in1=hmask[:, 16:32], op=ALU.mult), reads=[hmaskR, ubR], writes=[ubR])
                for c in range(4):
                    dg, dgR = dgs[c % 2]
                    for j in range(CONVW):
                        T.op("dve", lambda j=j, c=c: nc.vector.tensor_scalar(out=dg[:, j, :], in0=identb[:, :], scalar1=vB[:, c, 4 + j:5 + j], scalar2=None, op0=ALU.mult),
                             reads=[identbR, vBR], writes=[dgR], big=True)
                    for (bo, oc0, ns, cbo) in segs:
                        for o0 in range(0, ns, 512):
                            n = min(512, ns - o0)
                            b, bR = nb()
                            fns = [lambda j=j, b=b, n=n, o0=o0: nc.tensor.matmul(b[:, 0:n], lhsT=dg[:, j, :], rhs=ub[:, c, cbo + o0 + j:cbo + o0 + j + n], start=(j == 0), stop=(j == CONVW - 1))
                                   for j in range(CONVW)]
                            T.op("pe", fns, reads=[dgR, ubR], writes=[bR])
                            T.op("act", lambda b=b, n=n, o0=o0, oc0=oc0: nc.scalar.activation(out=cvs[:, c, oc0 + o0:oc0 + o0 + n], in_=b[:, 0:n], func=AF.Identity, bias=vB[:, c, 0:1]),
                                 reads=[bR, vBR], writes=[cvsR])
                for o0 in range(0, ntot, 512):
                    n = min(512, ntot - o0)
                    outs = [("act", lambda c, xn, o0=o0, n=n: act_affine(c1T[:, c, o0:o0 + n], xn, vB[:, c, 1:2], vB[:, c, 2:3], func=AF.Silu), [vBR], [c1TR])]
                    ln_fm(lnt3, cvs, cvsR, 4, o0, n, onesH, onesHR, outs)
                for g in range(4):
                    pz, pzR = pzb[0]
                    ic, icR = ict[0]
                    T.dma("sp", ic[:, 0:ntot], ic_d[g:g + 1, :].to_broadcast([128, ntot]), writes=[icR])
                    T.op("dve", lambda pz=pz: nc.vector.memset(pz[:, :], 0.0), writes=[pzR])
                    for (s0, n, bo) in in_groups:
                        def ev_pz(j, b, bR):
                            T.op("act", lambda: nc.scalar.copy(out=pz[:, bo:bo + n], in_=b[:, 0:n]), reads=[bR], writes=[pzR])
                        lin_fm(wC, wCR, 1024 + g * 128, 1, h, hR, s0, n, ev_pz)
                    if is_sample:
                        T.op("dve", lambda pz=pz: nc.vector.tensor_tensor(out=pz[:, 0:16], in0=pz[:, 0:16], in1=hmask[:, 0:16], op=ALU.mult), reads=[hmaskR, pzR], writes=[pzR])
                        T.op("dve", lambda pz=pz: nc.vector.tensor_tensor(out=pz[:, 16 + NOWN:32 + NOWN], in0=pz[:, 16 + NOWN:32 + NOWN], in1=hmask[:, 16:32], op=ALU.mult), reads=[hmaskR, pzR], writes=[pzR])
                    L = LB
                    srcb, srcR = pz, pzR
                    sh = 1
                    dst_list = [(Ta, TaR), (Tb, TbR)]
                    Lc = L
                    for k in range(g + 1):
                        dstb, dstR = dst_list[k % 2]
                        Lc = Lc - sh
                        T.op("dve", lambda srcb=srcb, dstb=dstb, Lc=Lc, sh=sh: nc.vector.tensor_tensor(out=dstb[:, 0:Lc], in0=srcb[:, 0:Lc], in1=srcb[:, sh:sh + Lc], op=ALU.add),
                             reads=[srcR], writes=[dstR])
                        srcb, srcR = dstb, dstR
                        sh *= 2
                    w = 2 ** (g + 1)
                    for (bo, oc0, ns, cbo) in segs:
                        for o0 in range(0, ns, 512):
                            n = min(512, ns - o0)
                            d_, dR_ = dtl[0]
                            a1, a1R = tA[0][0], tA[0][1]
                            T.op("dve", lambda: nc.vector.tensor_tensor(out=a1[:, 0:n], in0=srcb[:, bo + o0 - w // 2:bo + o0 - w // 2 + n], in1=ic[:, oc0 + o0:oc0 + o0 + n], op=ALU.mult),
                                 reads=[srcR, icR], writes=[a1R])
                            T.op("dve", lambda: nc.vector.tensor_tensor(out=d_[:, 0:n], in0=a1[:, 0:n], in1=pz[:, bo + o0:bo + o0 + n], op=ALU.subtract), reads=[a1R, pzR], writes=[dR_])
                            b, bR = nb()
                            T.op("pe", lambda b=b: nc.tensor.matmul(b[:, 0:n], lhsT=wpl[:, g, :], rhs=d_[:, 0:n], start=True, stop=True), reads=[dR_, wplR], writes=[bR])
                            T.op("act", lambda b=b: nc.scalar.activation(out=c1T[:, 4 + g, oc0 + o0:oc0 + o0 + n], in_=b[:, 0:n], func=AF.Copy, scale=vB[:, g, 3:4]), reads=[bR, vBR], writes=[c1TR])

            cvs, cvsR = sb(l1, "cvs", [128, 4, NOWN], F32)

            def mixer1_out(xr, xrR, hdst, hdstR, out_groups, cd):
                for (cc0, x0, n) in out_groups:
                    def ev_y(j, b, bR):
                        T.op("dve", lambda: nc.vector.scalar_tensor_tensor(out=xr[:, j, x0:x0 + n], in0=b[:, 0:n], scalar=mods[:, 1, 16 + j:17 + j, cd], in1=xr[:, j, x0:x0 + n],
                                                                          op0=ALU.mult, op1=ALU.add), reads=[bR, modsR, xrR], writes=[xrR], big=(n >= 256))
                    lin_fm(wco, wcoR, 0, 8, c1T, c1TR, cc0, n, ev_y)
                    outs = [
                        ("act", lambda c, xn: act_affine(hdst[:, c, x0:x0 + n], xn, cols[:, CI(1, cd, 0), c:c + 1], cols[:, CI(1, cd, 1), c:c + 1]), [colsR], [hdstR]),
                        ("dve", lambda c, xn: dve_affine(xr[:, c, x0:x0 + n], xn, cols[:, CI(1, cd, 4), c:c + 1], cols[:, CI(1, cd, 5), c:c + 1]), [colsR], [xrR]),
                    ]
                    ln_fm(lnt3, xr, xrR, 8, x0, n, onesD, onesDR, outs)

            conv_pool(hP, hPR, xrP, xrPR, 0, [(0, 256, 16), (256, 256, 16 + 256 + 16)],
                      [(16, 0, 256, 1), (16 + 256 + 16, 256, 256, 1 + 256 + 16)], icP_d, NP, False)
            mixer1_out(xrP, xrPR, hP, hPR, [(0, 0, 512)], 0)
            conv_pool(hS, hSR, xrS, xrSR, 1, [(0, 352, 0), (352, 352, 352), (704, 352, 704)],
                      [(16, 0, NOWN, 1)], icS_d, NOWN, True)
            mixer1_out(xrS, xrSR, hS, hSR, [(0, 16, 512), (512, 528, 512)], 1)

        T.barrier()
        stage("mix1", [("xrP", xrP[:].rearrange("p a b -> p (a b)"), [128, 8 * NP], [xrPR]), ("xrS", xrS[:].rearrange("p a b -> p (a b)"), [128, 8 * NQ], [xrSR])])
        mlp_layer(1, [(xrP, xrPR, hP, hPR, 0, 512, 0), (xrS, xrSR, hS, hSR, 16, 512, 1), (xrS, xrSR, hS, hSR, 528, 512, 1)], True)

        T.barrier()
        with contextlib.ExitStack() as fo:
            ot = [sb(fo, "ot%d" % i, [128, D], F32) for i in range(3)]
            k = 0
            for (xr, xrR, x0, ntok, dst) in ((xrP, xrPR, 0, NP, yp_d), (xrS, xrSR, 16, NOWN, ys_d)):
                for t in range(ntok // 128):
                    o, oR = ot[k % 3]
                    k += 1
                    for half in range(2):
                        b, bR = nb()
                        fns = [lambda c=c, b=b: nc.tensor.transpose(out=b[:, (c % 4) * 128:(c % 4 + 1) * 128], in_=xr[:, c, x0 + t * 128:x0 + (t + 1) * 128], identity=ident[:, :])
                               for c in range(half * 4, half * 4 + 4)]
                        T.op("pe", fns, reads=[xrR, identR], writes=[bR])
                        if half == 0:
                            T.op("act", lambda b=b, o=o: nc.scalar.copy(out=o[:, 0:512], in_=b[:, 0:512]), reads=[bR], writes=[oR])
                        else:
                            T.op("dve", lambda b=b, o=o: nc.vector.tensor_copy(out=o[:, 512:1024], in_=b[:, 0:512]), reads=[bR], writes=[oR])
                    T.dma("sp", dst[t * 128:(t + 1) * 128, :], o[:, :], reads=[oR], writes=[], is_output=True)
        T.finish()
    return nc


def mk_ln_tmps(stack):
    return None


def _rope_tables(pos, nrows, half):
    T_ = pos.shape[0]
    row = (pos // 64).astype(np.float64)
    col = (pos % 64).astype(np.float64)
    inv = 10000.0 ** (-np.arange(half, dtype=np.float64) / float(half))
    out = np.zeros((2, nrows, T_), np.float32)
    blk = 4 * half
    for i in range(nrows):
        d = i % blk
        p = row if d < 2 * half else col
        kk = d % half
        ang = p * inv[kk]
        sgn = -1.0 if (d % (2 * half)) < half else 1.0
        out[0, i] = np.cos(ang)
        out[1, i] = sgn * np.sin(ang)
    return out


def _perm(n, half):
    m = np.zeros((n, n), np.float32)
    for i in range(n):
        j = i + half if (i % (2 * half)) < half else i - half
        m[j, i] = 1.0
    return m


def _invcnt(T_, pos):
    out = np.zeros((4, pos.shape[0]), np.float32)
    for g, w in enumerate((2, 4, 8, 16)):
        lo = np.maximum(pos - w // 2, 0)
        hi = np.minimum(pos + w // 2 - 1, T_ - 1)
        out[g] = (1.0 / (hi - lo + 1).astype(np.float64)).astype(np.float32)
    return out


_NC_CACHE = {}


def kernel(x_prompt, x_sample, cache_diff_k, cache_diff_v, cache_mla_ckv, cache_mla_krope, c, c_ctx,
           w_ada, b_ada, ln_mix_g, ln_mix_b, ln_mlp_g, ln_mlp_b, w_mlp_in, w_mlp_out,
           w_att_in, w_uq, w_ukv, q_norm_g, kv_norm_g, lam_q1, lam_k1, lam_q2, lam_k2, diff_norm_g, w_att_out,
           w_conv_in, conv_w, conv_b, conv_norm_g, conv_norm_b, w_pool, pool_scale, w_conv_out):
    f = lambda a: np.ascontiguousarray(np.asarray(a, dtype=np.float32))
    x_prompt, x_sample = f(x_prompt), f(x_sample)
    if "nc" not in _NC_CACHE:
        _NC_CACHE["nc"] = build()
    nc = _NC_CACHE["nc"]
    pos_all = np.arange(TS)
    ropek = _rope_tables(pos_all, 128, 16)
    ropekpe = _rope_tables(pos_all, 32, 8)
    permd = _perm(128, 16)
    permpe = _perm(32, 8)
    ident = np.eye(128, dtype=np.float32)
    vecB = np.concatenate([f(conv_b)[0:1], f(conv_norm_g)[0:1], f(conv_norm_b)[0:1], f(pool_scale)[0:1], f(conv_w)[0]], axis=0)
    vecC = np.zeros((3, 256), np.float32)
    vecC[0] = f(q_norm_g)[0]
    vecC[1, :128] = f(kv_norm_g)[0]
    vecC[2, :128] = f(diff_norm_g)[0]
    lamv = np.concatenate([f(lam_q1)[0], f(lam_k1)[0], f(lam_q2)[0], f(lam_k2)[0]])[None, :]
    icP = np.concatenate([_invcnt(SEQ, np.arange(SEQ))] * 2, axis=1)
    shared = {
        "vecB": vecB, "vecC": vecC, "lamv": lamv, "ident": ident, "permd": permd, "permpe": permpe,
        "ropek": ropek, "ropekpe": ropekpe, "icP": icP,
        "w_ada": f(w_ada), "w_mlp_in": f(w_mlp_in), "w_mlp_out": f(w_mlp_out), "w_att_in": f(w_att_in)[0],
        "w_uq": f(w_uq)[0], "w_ukv": f(w_ukv)[0], "w_att_out": f(w_att_out)[0], "w_conv_in": f(w_conv_in)[0],
        "w_pool": f(w_pool)[0], "w_conv_out": f(w_conv_out)[0],
    }
    in_maps = []
    for core in range(NCORES):
        bs, qt = core // 4, core % 4
        own0 = qt * NOWN
        xq = np.zeros((NQ, D), np.float32)
        lo, hi = own0 - HALO, own0 + NOWN + HALO
        slo, shi = max(lo, 0), min(hi, TS)
        xq[slo - lo:shi - lo] = x_sample[bs, slo:shi]
        posq = np.clip(np.arange(lo, hi), 0, TS - 1)
        hm = np.zeros((1, 32), np.float32)
        hm[0, :16] = 1.0 if qt > 0 else 0.0
        hm[0, 16:] = 1.0 if qt < 3 else 0.0
        vecA = np.concatenate([f(b_ada)[0].reshape(6, D), f(b_ada)[1].reshape(6, D),
                               f(ln_mix_g)[0:1], f(ln_mix_b)[0:1], f(ln_mlp_g)[0:1], f(ln_mlp_b)[0:1],
                               f(ln_mix_g)[1:2], f(ln_mix_b)[1:2], f(ln_mlp_g)[1:2], f(ln_mlp_b)[1:2],
                               f(c)[bs:bs + 1], f(c_ctx)[None, :]], axis=0)
        m = dict(shared)
        m.update({
            "xp": x_prompt[2 * core:2 * core + 2].reshape(NP, D),
            "xq": xq,
            "xkv": x_sample[bs],
            "ck": f(cache_diff_k)[bs, 0].reshape(PAST, 512),
            "cv": f(cache_diff_v)[bs, 0].reshape(PAST, 512),
            "cckv": f(cache_mla_ckv)[bs, 0],
            "ckr": f(cache_mla_krope)[bs, 0],
            "vecA": vecA,
            "ropeq": np.ascontiguousarray(ropek[:, :, posq]),
            "ropeqpe": np.ascontiguousarray(ropekpe[:, :, posq]),
            "hmask": hm,
            "icS": _invcnt(TS, np.arange(own0, own0 + NOWN)),
        })
        in_maps.append({k: np.ascontiguousarray(v) for k, v in m.items()})
    if _NC_CACHE.get("prep_only"):
        return in_maps
    res = run_bass_kernel_spmd(nc, in_maps, core_ids=list(range(NCORES)))
    rs = res.results
    y_prompt = np.stack([rs[i]["yp"] for i in range(NCORES)]).reshape(16, SEQ, D)
    y_sample = np.stack([rs[i]["ys"] for i in range(NCORES)]).reshape(2, TS, D)
    ndk = np.stack([rs[i]["ndk"] for i in range(NCORES)]).reshape(16, 1, SEQ, 4, 128)
    ndv = np.stack([rs[i]["ndv"] for i in range(NCORES)]).reshape(16, 1, SEQ, 4, 128)
    nckv = np.stack([rs[i]["nckv"] for i in range(NCORES)]).reshape(16, 1, SEQ, 128)
    nkr = np.stack([rs[i]["nkr"] for i in range(NCORES)]).reshape(16, 1, SEQ, 32)
    return (y_prompt.astype(np.float32), y_sample.astype(np.float32), ndk.astype(np.float32), ndv.astype(np.float32),
            nckv.astype(np.float32), nkr.astype(np.float32))
```
